# Optimizing a Trainium2 kernel written in Bass

```python
import jax, jax.numpy as jnp
from jax import lax
import numpy as np

D_MODEL = 4096
BATCH = 2
SEQ = 8192
DEPTH = 1

N_MEM = 256
MEM_HEADS = 4
MEM_W = D_MODEL // 4
MEM_HD = MEM_W // MEM_HEADS
CONV_CH = D_MODEL // 4
CONV_WIDTH = 31
GLA_HEADS = 4
GLA_VAL = D_MODEL // 2
GLA_DV = GLA_VAL // GLA_HEADS
GLA_DK = GLA_DV // 2
GLA_KEY = GLA_HEADS * GLA_DK
GATE_RANK = 16
GATE_TAU = 16.0
CHUNK = 64
D_MIX = CONV_CH + GLA_VAL + MEM_W
SPLITS = (CONV_CH, CONV_CH, CONV_CH,
          GLA_KEY, GLA_KEY, GLA_VAL, GATE_RANK, GLA_VAL,
          MEM_W, MEM_W)
D_IN = sum(SPLITS)
LN_EPS = 1e-5
DN_ALPHA = (2 * DEPTH) ** 0.25
DN_BETA = (8 * DEPTH) ** -0.25

kernel_name = 'hybrid_conv_gla_memory_deepnorm'


def _layernorm(x, g, b):
    xf = x.astype(jnp.float32)
    mu = jnp.mean(xf, axis=-1, keepdims=True)
    var = jnp.mean(jnp.square(xf - mu), axis=-1, keepdims=True)
    return ((xf - mu) * lax.rsqrt(var + LN_EPS) * g + b).astype(x.dtype)


def _conv_group(u_a, u_b, dw, dw_b, ln_g, ln_b):
    h = u_a * jax.nn.sigmoid(u_b)
    h = lax.conv_general_dilated(
        h, dw[:, None, :].astype(h.dtype), window_strides=(1,),
        padding=[(CONV_WIDTH - 1, 0)],
        dimension_numbers=('NWC', 'WIO', 'NWC'),
        feature_group_count=CONV_CH) + dw_b
    h = _layernorm(h, ln_g, ln_b)
    return jax.nn.silu(h)


def _gla_group(q, k, v, lr, w_gate, gate_b, norm_g):
    B, S, _ = q.shape
    n_chunks = S // CHUNK
    log_a = jax.nn.log_sigmoid((lr @ w_gate + gate_b).astype(jnp.float32)) / GATE_TAU

    def chunks(t, d):
        return t.astype(jnp.float32).reshape(B, n_chunks, CHUNK, GLA_HEADS, d).transpose(1, 0, 3, 2, 4)

    qc = chunks(q, GLA_DK) * (GLA_DK ** -0.5)
    kc = chunks(k, GLA_DK)
    vc = chunks(v, GLA_DV)
    bcum = jnp.cumsum(chunks(log_a, GLA_DK), axis=3)
    b_last = bcum[:, :, :, -1:, :]
    q_t = qc * jnp.exp(bcum)
    k_t = kc * jnp.exp(-bcum)
    k_s = kc * jnp.exp(b_last - bcum)

    mask = jnp.tril(jnp.ones((CHUNK, CHUNK), dtype=bool))
    attn = jnp.where(mask, jnp.einsum('nbhcd,nbhsd->nbhcs', q_t, k_t), 0.0)
    o_intra = jnp.einsum('nbhcs,nbhse->nbhce', attn, vc)

    def step(state, inp):
        qt, ks, vv, bl = inp
        o = jnp.einsum('bhcd,bhde->bhce', qt, state)
        state = state * jnp.exp(bl)[:, :, 0, :, None] + jnp.einsum('bhcd,bhce->bhde', ks, vv)
        return state, o

    s0 = jnp.zeros((B, GLA_HEADS, GLA_DK, GLA_DV), jnp.float32)
    _, o_inter = lax.scan(step, s0, (q_t, k_s, vc, b_last))
    o = o_intra + o_inter
    o = o * lax.rsqrt(jnp.mean(jnp.square(o), axis=-1, keepdims=True) + LN_EPS) * norm_g
    return o.transpose(1, 0, 3, 2, 4).reshape(B, S, GLA_VAL).astype(v.dtype)


def _memory_group(q, mem_kv):
    B, S, _ = q.shape
    M = mem_kv.shape[1]
    mk, mv = jnp.split(mem_kv, 2, axis=-1)
    qh = q.reshape(B, S, MEM_HEADS, MEM_HD)
    kh = mk.reshape(B, M, MEM_HEADS, MEM_HD)
    vh = mv.reshape(B, M, MEM_HEADS, MEM_HD)
    scores = jnp.einsum('bshd,bmhd->bhsm', qh, kh).astype(jnp.float32) * (MEM_HD ** -0.5)
    p = jax.nn.softmax(scores, axis=-1).astype(vh.dtype)
    return jnp.einsum('bhsm,bmhd->bshd', p, vh).reshape(B, S, MEM_W)


def setup_inputs(seed: int = 0) -> dict:
    key = jax.random.key(seed)
    ks = jax.random.split(key, 14)
    nrm = lambda k, shape, s: jax.random.normal(k, shape, jnp.float32) * s
    return {
        'x': nrm(ks[0], (BATCH, SEQ, D_MODEL), 1.0),
        'mem': nrm(ks[1], (BATCH, N_MEM, D_MODEL), 1.0),
        'w_in': nrm(ks[2], (DEPTH, D_MODEL, D_IN), D_MODEL ** -0.5),
        'conv_dw': nrm(ks[3], (DEPTH, CONV_WIDTH, CONV_CH), CONV_WIDTH ** -0.5),
        'conv_dw_b': nrm(ks[4], (DEPTH, CONV_CH), 0.01),
        'conv_ln_g': 1.0 + nrm(ks[5], (DEPTH, CONV_CH), 0.01),
        'conv_ln_b': nrm(ks[6], (DEPTH, CONV_CH), 0.01),
        'gla_w_gate': nrm(ks[7], (DEPTH, GATE_RANK, GLA_KEY), GATE_RANK ** -0.5),
        'gla_gate_b': nrm(ks[8], (DEPTH, GLA_KEY), 0.01),
        'gla_norm_g': 1.0 + nrm(ks[9], (DEPTH, GLA_DV), 0.01),
        'w_mem_kv': nrm(ks[10], (DEPTH, D_MODEL, 2 * MEM_W), D_MODEL ** -0.5),
        'w_out': nrm(ks[11], (DEPTH, D_MIX, D_MODEL), (D_MIX ** -0.5) * DN_BETA),
        'ln_g': 1.0 + nrm(ks[12], (DEPTH, D_MODEL), 0.01),
        'ln_b': nrm(ks[13], (DEPTH, D_MODEL), 0.01),
    }


def reference(x, mem, w_in, conv_dw, conv_dw_b, conv_ln_g, conv_ln_b, gla_w_gate,
              gla_gate_b, gla_norm_g, w_mem_kv, w_out, ln_g, ln_b):
    split_idx = [int(i) for i in np.cumsum(SPLITS)[:-1]]
    for l in range(DEPTH):
        proj = x @ w_in[l]
        (c_a, c_b, c_gate, g_q, g_k, g_v, g_lr, g_gate, m_q, m_gate) = jnp.split(proj, split_idx, axis=-1)
        y_conv = _conv_group(c_a, c_b, conv_dw[l], conv_dw_b[l], conv_ln_g[l], conv_ln_b[l]) * jax.nn.silu(c_gate)
        y_gla = _gla_group(g_q, g_k, g_v, g_lr, gla_w_gate[l], gla_gate_b[l], gla_norm_g[l]) * jax.nn.silu(g_gate)
        y_mem = _memory_group(m_q, mem @ w_mem_kv[l]) * jax.nn.silu(m_gate)
        mixed = jnp.concatenate([y_conv, y_gla, y_mem], axis=-1)
        out = mixed @ w_out[l]
        x = _layernorm(DN_ALPHA * x + out, ln_g[l], ln_b[l])
    return x
```

```python
from contextlib import ExitStack
import numpy as np
import concourse.bass as bass
import concourse.mybir as mybir
from concourse.bass_utils import run_bass_kernel_spmd

F32 = mybir.dt.float32
BF16 = mybir.dt.bfloat16
AF = mybir.ActivationFunctionType
ALU = mybir.AluOpType
AX = mybir.AxisListType

EPOCH = 8000
D = 4096
D_IN = 11280
NQ = 2048
HALF = 1024
XC = 32
XW = XC + HALF
OFF_CA, OFF_CB, OFF_CG, OFF_Q, OFF_K, OFF_V, OFF_LR, OFF_GG, OFF_MQ, OFF_MG = 0, 1024, 2048, 3072, 4096, 5120, 7168, 7184, 9232, 10256
ALPHA = 2.0 ** 0.25
LN_EPS = 1e-5


class Prog:
    ENG = ("pe", "act", "dve", "pool", "sp")

    def __init__(self, nc):
        self.nc = nc
        self.ops = []
        self.last_write = {}
        self.readers = {}
        self.last_on = {}
        self.dmas_since_bar = []
        self.bar = None
        self.bar_done = set()

    def barrier(self):
        b = set(self.last_on.values())
        b.update(self.dmas_since_bar)
        self.bar = b
        self.bar_done = set()
        self.dmas_since_bar = []

    def op(self, eng, fn, reads=(), writes=(), *args, dma=False, sem_key=None, cc=False, **kw):
        if cc:
            dma = "cc"
            sem_key = sem_key or ("cc", len(self.ops))
        if isinstance(fn, str):
            fn = (fn, args, kw)
        deps = set()
        for k in reads:
            w = self.last_write.get(k)
            if w is not None:
                deps.add(w)
        for k in writes:
            w = self.last_write.get(k)
            if w is not None:
                deps.add(w)
            r = self.readers.get(k)
            if r:
                deps.update(r.values())
        if self.bar is not None and eng not in self.bar_done:
            deps.update(self.bar)
            self.bar_done.add(eng)
        idx = len(self.ops)
        if dma and sem_key is None:
            sem_key = writes[0]
        self.ops.append([eng, fn, deps, dma, sem_key])
        rk = ("d", idx) if dma else eng
        for k in reads:
            self.readers.setdefault(k, {})[rk] = idx
        for k in writes:
            self.last_write[k] = idx
            self.readers[k] = {}
        if dma:
            self.dmas_since_bar.append(idx)
        else:
            self.last_on[eng] = idx
        return idx

    def dma(self, eng, out, in_, reads=(), writes=(), sem_key=None):
        return self.op(eng, "dma_start", reads, writes, dma=True, sem_key=sem_key, out=out, in_=in_)

    def emit(self, block, stack):
        nc = self.nc
        ops = self.ops
        n = len(ops)
        needed = [False] * n
        seen = {e: {} for e in self.ENG}
        final_deps = [None] * n
        for i, (eng, fn, deps, dma, sk) in enumerate(ops):
            per = {}
            fd = []
            s = seen[eng]
            for d in deps:
                de, _, _, ddma, _ = ops[d]
                if ddma:
                    key = ("d", d)
                    if key in s:
                        continue
                    s[key] = True
                    fd.append(d)
                else:
                    if de == "pe" and eng == "pe" and not dma:
                        continue
                    if per.get(de, -1) < d:
                        per[de] = d
            for de, d in per.items():
                if s.get(de, -1) >= d:
                    continue
                s[de] = d
                fd.append(d)
            final_deps[i] = fd
            for d in fd:
                needed[d] = True
        cnt = {e: 0 for e in self.ENG}
        eng_sems = {e: [] for e in self.ENG}
        dma_sems = {}
        dma_cnt = {}
        token = [None] * n
        for i, (eng, fn, deps, dma, sk) in enumerate(ops):
            if dma:
                if sk not in dma_sems:
                    dma_sems[sk] = stack.enter_context(nc.semaphore("dq%d" % len(dma_sems)))
                    dma_cnt[sk] = 0
                inc = 1 if dma == "cc" else 16
                dma_cnt[sk] += inc
                token[i] = (dma_sems[sk], dma_cnt[sk], inc)
            elif needed[i]:
                c = cnt[eng]
                ep = c // EPOCH
                while len(eng_sems[eng]) <= ep:
                    eng_sems[eng].append(stack.enter_context(nc.semaphore("s_%s%d" % (eng, len(eng_sems[eng])))))
                cnt[eng] = c + 1
                token[i] = (eng_sems[eng][ep], c % EPOCH + 1, 1)
        self.n_sems = len(dma_sems) + sum(len(v) for v in eng_sems.values())
        per_eng = {e: [] for e in self.ENG}
        for i, o in enumerate(ops):
            per_eng[o[0]].append(i)

        def run(engname, e):
            for i in per_eng[engname]:
                fn = ops[i][1]
                for d in final_deps[i]:
                    sem, val, _ = token[d]
                    e.wait_ge(sem, val)
                if fn is not None:
                    ins = getattr(e, fn[0])(*fn[1], **fn[2])
                    if token[i] is not None:
                        ins.then_inc(token[i][0], token[i][2])

        block.tensor(lambda e: run("pe", e))
        block.scalar(lambda e: run("act", e))
        block.vector(lambda e: run("dve", e))
        block.gpsimd(lambda e: run("pool", e))
        block.sync(lambda e: run("sp", e))


def build(n_pre=0, n_main=2, dbg=False):
    nc = bass.Bass("TRN2", target_bir_lowering=False, num_devices=8)
    dram_in = lambda name, shape, dt=F32: nc.dram_tensor(name, list(shape), dt, kind="ExternalInput").ap()
    xall = dram_in("xall", [XC + NQ, D])
    memd = dram_in("mem", [256, D])
    w_in = dram_in("w_in", [D, D_IN])
    w_mkv = dram_in("w_mkv", [D, 2048])
    w_out = dram_in("w_out", [D, D])
    prm = dram_in("prm", [35, 1024])
    wga = dram_in("wga", [17, 1024])
    lngb = dram_in("lngb", [2, D])
    identd = dram_in("identd", [128, 128])
    m2d = dram_in("m2", [128, 256])
    cmaskd = dram_in("cmaskd", [128, 128])
    out = nc.dram_tensor("out", [NQ, D], F32, kind="ExternalOutput").ap()
    mixs_all = nc.dram_tensor("mixs", [2 * D, HALF], BF16).ap()
    osc = nc.dram_tensor("osc", [8 * HALF, 512], BF16).ap()
    ngsc = nc.dram_tensor("ngsc", [8 * HALF, 512], BF16).ap()
    qgsc = nc.dram_tensor("qgsc", [8 * 256, HALF], BF16).ap()
    xsrc = [nc.dram_tensor("xsrc%d" % i, [128, w], F32).ap() for i, w in enumerate((2048, 2048, 64))]
    xdst = [nc.dram_tensor("xdst%d" % i, [4 * 128, w], F32).ap() for i, w in enumerate((2048, 2048, 64))]
    seld = dram_in("sel", [128, 8])
    dbgt = nc.dram_tensor("dbg", [128, 4096], F32, kind="ExternalOutput").ap() if dbg else None

    st = ExitStack()
    with st:
        T = lambda name, shape, dt: st.enter_context(nc.sbuf_tensor(name, list(shape), dt))
        PT = lambda name, shape, dt: st.enter_context(nc.psum_tensor(name, list(shape), dt))
        xT = T("xT", [128, 32, XW], BF16)
        state32 = T("state32", [128, 4, 2, 512], F32)
        stateb = T("stateb", [128, 4, 2, 512], BF16)
        mkT = T("mkT", [128, 8, 256], BF16)
        mv = T("mv", [128, 2, 1024], BF16)
        identf = T("identf", [128, 128], F32)
        identb = T("identb", [128, 128], BF16)
        onesb = T("onesb", [128, 128], BF16)
        M2 = T("M2", [128, 256], F32)
        cmask = T("cmask", [128, 128], F32)
        wgs = T("wgs", [32, 1024], F32)
        lrT = T("lrT", [32, HALF], F32)
        convp = T("convp", [128, 8, 35], F32)
        prms = T("prms", [35, 1024], F32)
        cst = T("cst", [128, 4], F32)
        tailT = T("tailT", [128, 8, 32], BF16)
        ss = T("ss", [128, 16], F32)
        sm = T("sm", [128, 8], F32)
        s1h = [T("s1_%d" % i, [128, 128], F32) for i in range(2)]
        s2h = [T("s2_%d" % i, [128, 128], F32) for i in range(2)]
        lst = T("lst", [128, 8], F32)
        lsu = T("lsu", [128, 12], F32)
        carry = T("carry", [128, 64], F32)
        cpA = T("cpA", [128, 2, 16], F32)
        cpB = T("cpB", [128, 2, 16], F32)
        PBf = T("PBf", [128, 2, 16], F32)
        selt = T("selt", [128, 8], F32)
        pq = T("pq", [128, 3, 8], F32)
        wts = T("wts", [128, 4, 8], F32)
        wbuf = [T("wbuf%d" % i, [128, 32, 128], BF16) for i in range(2)]
        NAR = 18432
        arena = T("arena", [128, NAR], F32)
        G = [PT("G%d" % i, [128, 512], F32) for i in range(2)]
        M = [PT("PM%d" % i, [128, 512], F32) for i in range(4)]
        TB = [PT("TB%d" % i, [128, 8, 128], BF16) for i in range(2)]
        block = st.enter_context(nc.Block())
        P = Prog(nc)
        cnt = {"w": 0, "g": 0, "tb": 0, "ev": 0}

        def af(o, n):
            assert o + n <= NAR, (o, n)
            return arena[:, o:o + n]

        def ab(o, n):
            assert o % 2 == 0 and n % 2 == 0 and (o + n) // 2 <= NAR, (o, n)
            return arena[:, o // 2:(o + n) // 2].bitcast(BF16)

        ACT = lambda fn, r, w, *a, **k: P.op("act", fn, r, w, *a, **k)
        DVE = lambda fn, r, w, *a, **k: P.op("dve", fn, r, w, *a, **k)
        PE = lambda fn, r, w, *a, **k: P.op("pe", fn, r, w, *a, **k)
        POOL = lambda fn, r, w, *a, **k: P.op("pool", fn, r, w, *a, **k)

        def next_tb():
            i = cnt["tb"] % 2
            cnt["tb"] += 1
            return TB[i], "TB%d" % i

        def evac_eng():
            cnt["ev"] += 1
            return "dve" if cnt["ev"] % 2 else "act"

        def copy_op(eng, o, i, r, w):
            if eng == "act":
                P.op("act", "activation", r, w, out=o, in_=i, func=AF.Copy)
            else:
                P.op(eng, "tensor_copy", r, w, out=o, in_=i)

        P.dma("sp", identf[:], identd, writes=["identf"])
        P.dma("sp", M2[:], m2d, writes=["M2c"])
        P.dma("sp", cmask[:], cmaskd, writes=["cmask"])
        P.dma("sp", wgs[0:17, :], wga, writes=["wgs"])
        P.dma("sp", prms[:], prm, writes=["prms"])
        DVE("tensor_copy", ["identf"], ["identb"], out=identb[:], in_=identf[:])
        DVE("memset", [], ["onesb"], onesb[:], 1.0)
        DVE("memset", [], ["lrT"], lrT[:], 1.0)
        DVE("memset", [], ["cst"], cst[:, 0:1], 1.0)
        DVE("memset", [], ["cst"], cst[:, 1:2], LN_EPS)
        DVE("memset", [], ["cst"], cst[:, 2:3], 0.0)
        DVE("memset", [], ["st32"], state32[:], 0.0)
        DVE("memset", [], ["stb"], stateb[:], 0.0)
        DVE("memset", [], ["tailT"], tailT[:], 0.0)
        DVE("memset", [], ["carry"], carry[:], 1.0)
        P.dma("sp", selt[:], seld, writes=["selt"])
        for cc in range(8):
            PE("matmul", ["prms", "identf"], ["M0"], M[0][:, cc * 35:(cc + 1) * 35], lhsT=prms[:, cc * 128:(cc + 1) * 128], rhs=identf[0:35, 0:35], start=True, stop=True)
        DVE("tensor_copy", ["M0"], ["convp"], out=convp[:], in_=M[0][:, 0:280].rearrange("p (a b) -> p a b", a=8))

        def build_xT(src_rows, n_tiles, col0, rows_per=128):
            xs2 = [af(0, 4096), af(4096, 4096)]
            xb2 = [ab(16384, 4096), ab(20480, 4096)]
            for i in range(n_tiles):
                xs, xb = xs2[i % 2], xb2[i % 2]
                xsk, xbk = "xs%d" % (i % 2), "xb%d_" % (i % 2)
                r0 = src_rows + i * rows_per
                P.dma("sp", xs[0:rows_per, :], xall[r0:r0 + rows_per, :] if src_rows >= 0 else memd[i * rows_per:(i + 1) * rows_per, :],
                      writes=[xsk])
                ACT("activation", [xsk], [xbk + "0"], out=xb[0:rows_per, 0:2048], in_=xs[0:rows_per, 0:2048], func=AF.Copy)
                DVE("tensor_copy", [xsk], [xbk + "1"], out=xb[0:rows_per, 2048:4096], in_=xs[0:rows_per, 2048:4096])
                for g in range(4):
                    tb, tbk = next_tb()
                    for j in range(8):
                        kc = g * 8 + j
                        PE("transpose", [xbk + ("0" if kc < 16 else "1"), "identb"], [tbk], out=tb[:, j, 0:rows_per], in_=xb[0:rows_per, kc * 128:(kc + 1) * 128],
                                                                   identity=identb[0:rows_per, 0:rows_per])
                    c0 = col0 + i * rows_per
                    copy_op(evac_eng(), xT[:, g * 8:(g + 1) * 8, c0:c0 + rows_per], tb[:, :, 0:rows_per], [tbk], ["xT"])

        def proj_tasks(wsrc, col0, width, blocks, evac):
            stt = {}

            def dma_fn():
                if "wb" in stt:
                    return
                slot = cnt["w"] % 2
                cnt["w"] += 1
                stt["wb"] = wbuf[slot]
                stt["wk"] = "wbuf%d" % slot
                wv = wsrc[:, col0:col0 + width].rearrange("(kc p) n -> p kc n", p=128)
                for g4 in range(4):
                    P.dma("pool", stt["wb"][:, g4 * 8:(g4 + 1) * 8, 0:width], wv[:, g4 * 8:(g4 + 1) * 8, :], writes=["%s_%d" % (stt["wk"], g4)])

            def mk(bi, c0, n):
                loc = {}

                def part(p):
                    def f():
                        if p == 0:
                            dma_fn()
                            loc["gi"] = cnt["g"] % 2
                            cnt["g"] += 1
                        wb, wk = stt["wb"], stt["wk"]
                        gi = loc["gi"]
                        gk = "G%d" % gi
                        for kc in range(8 * p, 8 * p + 8):
                            PE("matmul", ["%s_%d" % (wk, kc // 8), "xT"], [gk], G[gi][0:width, 0:n], lhsT=wb[:, kc, 0:width], rhs=xT[:, kc, c0:c0 + n],
                               start=(kc == 0), stop=(kc == 31))
                        if p == 3:
                            evac(G[gi][0:width, 0:n], bi, gk)
                    return f
                parts = [part(p) for p in range(4)]

                def task():
                    for f in parts:
                        f()
                task.parts = parts
                return task
            return dma_fn, [mk(bi, c0, n) for bi, (c0, n) in enumerate(blocks)]

        def flat_parts(tasks):
            return [p for t in tasks for p in t.parts]

        def proj(wsrc, col0, width, blocks, evac):
            d, tasks = proj_tasks(wsrc, col0, width, blocks, evac)
            for t in tasks:
                t()

        MAINB = [(XC, 512), (XC + 512, 512)]

        build_xT(-1, 2, XC)
        for c in range(8):
            def ev(ps, bi, gk, c=c):
                DVE("tensor_copy", [gk], ["mkT"], out=mkT[:, c, :], in_=ps)
            proj(w_mkv, c * 128, 128, [(XC, 256)], ev)
        mvtmp = ab(26000, 256)
        for c in range(8):
            def ev(ps, bi, gk, c=c):
                DVE("tensor_copy", [gk], ["mvtmp"], out=mvtmp, in_=ps)
                tb, tbk = next_tb()
                for mc in range(2):
                    PE("transpose", ["mvtmp", "identb"], [tbk], out=tb[:, mc, :], in_=mvtmp[:, mc * 128:(mc + 1) * 128], identity=identb[:])
                DVE("tensor_copy", [tbk], ["mv"], out=mv[:, :, c * 128:(c + 1) * 128], in_=tb[:, 0:2, :])
            proj(w_mkv, 1024 + c * 128, 128, [(XC, 256)], ev)
        P.barrier()

        O_QZ, O_KT, O_KS, O_KSZ, O_VT, O_NG, O_YT, O_QT = 0, 4096, 6144, 8192, 12288, 16384, 20480, 24576
        O_TMP = 26624
        F_SP = 14464 + 64
        qz = ab(O_QZ, 4096).rearrange("p (i a d c) -> p i a d c", i=8, a=2, d=2)
        kT = ab(O_KT, 2048).rearrange("p (d t) -> p d t", d=2)
        ksT = ab(O_KS, 2048).rearrange("p (d t) -> p d t", d=2)
        ksz = ab(O_KSZ, 4096).rearrange("p (i a f) -> p i a f", i=8, a=2)
        vtok = ab(O_VT, 4096).rearrange("p (i e) -> p i e", i=8)
        ngtok = ab(O_NG, 4096).rearrange("p (i e) -> p i e", i=8)
        yT = ab(O_YT, 4096).rearrange("p (e t) -> p e t", e=4)
        qT = ab(O_QT, 2048).rearrange("p (d t) -> p d t", d=2)
        ptmp = [ab(O_TMP + 512 * i, 512) for i in range(2)]
        attnb = [ab(O_TMP + 1024 + 128 * i, 128) for i in range(2)]
        ytok = [ab(O_TMP + 1280 + 512 * i, 512) for i in range(2)]
        spt = af(F_SP, 256)
        tmpf = af(F_SP + 256, 256)
        Ef = [af(F_SP + 512 + 768 * i, 768).rearrange("p (a d t) -> p a d t", a=3, d=2) for i in range(2)]
        dec = af(F_SP + 2048, 32).rearrange("p (d n) -> p d n", d=2)
        junkf = af(F_SP + 2080, 512)

        def gla_zero_pads():
            DVE("memset", [], ["qz"], ab(O_QZ, 4096), 0.0)
            DVE("memset", [], ["ksz"], ab(O_KSZ, 4096), 0.0)

        def lr_proj():
            def ev(ps, bi, gk):
                DVE("tensor_copy", [gk], ["lrT"], out=lrT[0:16, bi * 512:(bi + 1) * 512], in_=ps)
            proj(w_in, OFF_LR, 16, MAINB, ev)

        def k_proj(h):
            for dc in range(2):
                def ev(ps, bi, gk, dc=dc):
                    copy_op(evac_eng(), kT[:, dc, bi * 512:(bi + 1) * 512], ps, [gk], ["kT"])
                proj(w_in, OFF_K + h * 256 + dc * 128, 128, MAINB, ev)

        def v_tasks(h):
            out_ = []
            for ec in range(4):
                def ev(ps, bi, gk, ec=ec):
                    t = ptmp[cnt["ev"] % 2]
                    tk = "ptmp%d" % (cnt["ev"] % 2)
                    copy_op(evac_eng(), t, ps, [gk], [tk])
                    tb, tbk = next_tb()
                    for j in range(4):
                        PE("transpose", [tk, "identb"], [tbk], out=tb[:, j, :], in_=t[:, j * 128:(j + 1) * 128], identity=identb[:])
                    DVE("tensor_copy", [tbk], ["vtok"], out=vtok[:, bi * 4:(bi + 1) * 4, ec * 128:(ec + 1) * 128], in_=tb[:, 0:4, :])
                d, tk_ = proj_tasks(w_in, OFF_V + h * 512 + ec * 128, 128, MAINB, ev)
                out_.extend(tk_)
            return out_

        def kv_proj(h, main):
            k_proj(h)
            for t in v_tasks(h):
                t()

        def gla_prep_s1a(h, i, main):
            tc0 = i * 128
            PE("matmul", ["lrT", "wgs"], ["M0"], M[0][:, 0:256], lhsT=lrT[0:17, tc0:tc0 + 128], rhs=wgs[0:17, h * 256:(h + 1) * 256], start=True, stop=True)
            ACT("activation", ["M0"], ["tmpf"], out=tmpf, in_=M[0][:, 0:256], func=AF.Exp, scale=-1.0)
            ACT("activation", ["tmpf", "cst"], ["spt"], out=spt, in_=tmpf, func=AF.Ln, bias=cst[:, 0:1], scale=1.0)

        def gla_prep_s1b(h, i, main):
            mb, mbk = (M[1], "M1") if i % 2 == 0 else (M[3], "M3")
            for fc in range(2):
                PE("matmul", ["spt", "M2c"], [mbk], mb[:, fc * 256:(fc + 1) * 256], lhsT=spt[:, fc * 128:(fc + 1) * 128], rhs=M2[:], start=True, stop=True)

        def gla_prep_s2a(h, i, main):
            tc0 = i * 128
            mb, mbk = (M[1], "M1") if i % 2 == 0 else (M[3], "M3")
            E = Ef[i % 2]
            ek = "Ef%d" % (i % 2)
            m1v = mb[:, :].rearrange("p (d t) -> p d t", d=2)
            ACT("activation", [mbk], [ek], out=E[:, 0], in_=m1v[:, :, 0:128], func=AF.Exp)
            ACT("activation", [mbk], [ek], out=E[:, 2], in_=m1v[:, :, 128:256], func=AF.Exp)
            if main:
                ACT("activation", [mbk], [ek], out=E[:, 1], in_=m1v[:, :, 0:128], func=AF.Exp, scale=-1.0)
            DVE("tensor_copy", [ek], ["dec"], out=dec[:, :, 2 * i:2 * i + 2], in_=E[:, 0, :, 63:128:64])
            DVE("tensor_tensor", [ek, "kT"], ["ksT"], out=ksT[:, :, tc0:tc0 + 128], in0=kT[:, :, tc0:tc0 + 128], in1=E[:, 2], op=ALU.mult)
            if main:
                DVE("tensor_tensor", [ek, "kT"], ["kT"], out=kT[:, :, tc0:tc0 + 128], in0=kT[:, :, tc0:tc0 + 128], in1=E[:, 1], op=ALU.mult)
                for a in range(2):
                    DVE("scalar_tensor_tensor", [ek, "qT"], ["qz"], out=qz[:, i, a, :, a * 64:(a + 1) * 64], in0=qT[:, :, tc0 + a * 64:tc0 + (a + 1) * 64],
                        scalar=0.0625, in1=E[:, 0, :, a * 64:(a + 1) * 64], op0=ALU.mult, op1=ALU.mult)

        def gla_prep_s2b(h, i, main):
            tc0 = i * 128
            tb, tbk = next_tb()
            for fc in range(2):
                PE("transpose", ["ksT", "identb"], [tbk], out=tb[:, fc, :], in_=ksT[:, fc, tc0:tc0 + 128], identity=identb[:])
            tbv = tb[:, 0:2, :]
            DVE("tensor_copy", [tbk], ["ksz"], out=ksz[0:64, i, 0, :].rearrange("p (d f) -> p d f", d=2), in_=tbv[0:64])
            ACT("activation", [tbk], ["ksz"], out=ksz[64:128, i, 1, :].rearrange("p (d f) -> p d f", d=2), in_=tbv[64:128], func=AF.Copy)

        def gla_prep_all(h, main, tasks=()):
            parts = flat_parts(tasks)

            def pp():
                if parts:
                    parts.pop(0)()
            gla_prep_s1a(h, 0, main)
            gla_prep_s1b(h, 0, main)
            for i in range(8):
                if i + 1 < 8:
                    gla_prep_s1a(h, i + 1, main)
                pp()
                if i + 1 < 8:
                    gla_prep_s1b(h, i + 1, main)
                gla_prep_s2a(h, i, main)
                pp()
                pp()
                gla_prep_s2b(h, i, main)
                pp()
            while parts:
                pp()

        def state_update(h, i, a):
            n = 2 * i + a
            for dc in range(2):
                mb, mbk = (M[3], "M3") if dc == 0 else (M[1], "M1")
                PE("matmul", ["ksz", "vtok"], [mbk], mb[:, :], lhsT=ksz[:, i, a, dc * 128:(dc + 1) * 128], rhs=vtok[:, i, :], start=True, stop=True)
                DVE("scalar_tensor_tensor", [mbk, "dec", "st32"], ["st32"], out=state32[:, h, dc, :], in0=state32[:, h, dc, :], scalar=dec[:, dc, n:n + 1],
                                                           in1=mb[:, :], op0=ALU.mult, op1=ALU.add)
                ACT("activation", ["st32"], ["stb"], out=stateb[:, h, dc, :], in_=state32[:, h, dc, :], func=AF.Copy)

        for hp in range(n_pre):
            P.barrier()
            build_xT((6 - n_pre + hp) * HALF, 8, XC)
            P.barrier()
            lr_proj()
            gla_zero_pads()
            for h in range(4):
                kv_proj(h, False)
                gla_prep_all(h, False)
                for i in range(8):
                    for a in range(2):
                        state_update(h, i, a)
                P.barrier()

        def conv_phase(hm):
            mixs = mixs_all[hm * D:(hm + 1) * D, :]
            hT = ab(0, 8 * XW).rearrange("p (c t) -> p c t", c=8)
            diag = ab(8448, 31 * 128).rearrange("p (j c) -> p j c", j=31)
            sig = ab(12416, XW)
            sqt = [ab(13472 + 512 * i, 512) for i in range(2)]
            gt = [ab(14496 + 512 * i, 512) for i in range(2)]
            at_ = [ab(15520 + 512 * i, 512) for i in range(2)]
            ycv = ab(16544, 1024)
            rstd = af(8800, 1024)
            nmr = af(9824, 1024)
            mu = af(10848, 512)
            t2 = af(11360, 512)
            tf = [af(11872 + 512 * i, 512) for i in range(2)]
            blocks = MAINB + ([(0, XC)] if hm == 0 else [])
            accs = [af(4224, 1024), af(12896, 1024)]

            def conv_ops(cc):
                hk = "hT%d" % cc
                acc, ak = accs[cc % 2], "acc%d" % (cc % 2)
                ops = []
                for j in range(31):
                    def tap(j=j):
                        src = hT[:, cc, 2 + j:2 + j + HALF]
                        if j == 0:
                            DVE("tensor_scalar", [hk, "convp"], [ak], out=acc, in0=src, scalar1=convp[:, cc, 0:1], scalar2=None, op0=ALU.mult)
                        else:
                            DVE("scalar_tensor_tensor", [hk, "convp", ak], [ak], out=acc, in0=src, scalar=convp[:, cc, j:j + 1], in1=acc, op0=ALU.mult, op1=ALU.add)
                    ops.append(tap)

                def fin():
                    for b in range(2):
                        cv = hT[:, cc, XC + 512 * b:XC + 512 * (b + 1)]
                        ACT("activation", [ak, "convp"], [hk], out=cv, in_=acc[:, 512 * b:512 * (b + 1)], func=AF.Identity, bias=convp[:, cc, 31:32], scale=1.0)
                        ACT("activation", [hk], ["sqt%d" % b], out=sqt[b], in_=cv, func=AF.Square)
                        PE("matmul", [hk, "onesb"], ["M%d" % b], M[b][:, :], lhsT=onesb[:], rhs=cv, start=(cc == 0), stop=(cc == 7))
                        PE("matmul", ["sqt%d" % b, "onesb"], ["M%d" % (2 + b)], M[2 + b][:, :], lhsT=onesb[:], rhs=sqt[b], start=(cc == 0), stop=(cc == 7))
                ops.append(fin)
                return ops

            pending = []
            per_task = 6 if hm == 0 else 8
            for cc in range(8):
                def ev_b(ps, bi, gk):
                    c0, n = blocks[bi]
                    ACT("activation", [gk], ["sig"], out=sig[:, c0:c0 + n], in_=ps, func=AF.Sigmoid)

                def ev_a(ps, bi, gk, cc=cc):
                    c0, n = blocks[bi]
                    DVE("tensor_tensor", [gk, "sig"], ["hT%d" % cc], out=hT[:, cc, c0:c0 + n], in0=ps, in1=sig[:, c0:c0 + n], op=ALU.mult)
                tasks_ = proj_tasks(w_in, OFF_CB + cc * 128, 128, blocks, ev_b)[1] + proj_tasks(w_in, OFF_CA + cc * 128, 128, blocks, ev_a)[1]
                for t in tasks_:
                    t()
                    for _ in range(per_task):
                        if pending:
                            pending.pop(0)()
                while pending:
                    pending.pop(0)()
                hk = "hT%d" % cc
                if hm == 1:
                    DVE("tensor_copy", ["tailT"], [hk], out=hT[:, cc, 0:XC], in_=tailT[:, cc, :])
                DVE("tensor_copy", [hk], ["tailT"], out=tailT[:, cc, :], in_=hT[:, cc, HALF:XW])
                pending = conv_ops(cc)
            while pending:
                pending.pop(0)()
            for b in range(2):
                sl = slice(512 * b, 512 * (b + 1))
                DVE("tensor_scalar", ["M%d" % b], ["mu"], out=mu, in0=M[b][:, :], scalar1=1.0 / 1024, scalar2=None, op0=ALU.mult)
                DVE("tensor_tensor", ["mu"], ["t2"], out=t2, in0=mu, in1=mu, op=ALU.mult)
                DVE("scalar_tensor_tensor", ["M%d" % (2 + b), "t2"], ["t2"], out=t2, in0=M[2 + b][:, :], scalar=1.0 / 1024, in1=t2, op0=ALU.mult, op1=ALU.subtract)
                ACT("activation", ["t2", "cst"], ["t2"], out=t2, in_=t2, func=AF.Sqrt, bias=cst[:, 1:2], scale=1.0)
                DVE("reciprocal", ["t2"], ["rstd"], out=rstd[:, sl], in_=t2)
                DVE("scalar_tensor_tensor", ["mu", "rstd"], ["nmr"], out=nmr[:, sl], in0=mu, scalar=-1.0, in1=rstd[:, sl], op0=ALU.mult, op1=ALU.mult)
            for cc in range(8):
                def ev_g(ps, bi, gk, cc=cc):
                    sl = slice(512 * bi, 512 * (bi + 1))
                    cv = hT[:, cc, XC + 512 * bi:XC + 512 * (bi + 1)]
                    ACT("activation", [gk], ["gt%d" % bi], out=gt[bi], in_=ps, func=AF.Silu)
                    DVE("tensor_tensor", ["hT%d" % cc, "rstd"], ["tf%d" % bi], out=tf[bi], in0=cv, in1=rstd[:, sl], op=ALU.mult)
                    DVE("tensor_tensor", ["tf%d" % bi, "nmr"], ["tf%d" % bi], out=tf[bi], in0=tf[bi], in1=nmr[:, sl], op=ALU.add)
                    ACT("activation", ["tf%d" % bi, "convp"], ["at%d" % bi], out=at_[bi], in_=tf[bi], func=AF.Silu, scale=convp[:, cc, 32:33], bias=convp[:, cc, 33:34])
                    DVE("tensor_tensor", ["at%d" % bi, "gt%d" % bi], ["ycv"], out=ycv[:, sl], in0=at_[bi], in1=gt[bi], op=ALU.mult)
                proj(w_in, OFF_CG + cc * 128, 128, MAINB, ev_g)
                P.dma("sp", mixs[cc * 128:(cc + 1) * 128, :], ycv, reads=["ycv"], writes=["mixs"], sem_key="st_ycv")

        qg = ab(2 * 17120, 2048).rearrange("p (d t) -> p d t", d=2)

        def gla_head_main(h, hm):
            for dc in range(2):
                def ev(ps, bi, gk, dc=dc):
                    copy_op(evac_eng(), qT[:, dc, bi * 512:(bi + 1) * 512], ps, [gk], ["qT"])
                proj(w_in, OFF_Q + h * 256 + dc * 128, 128, MAINB, ev)
            k_proj(h)
            gtasks = []
            gdmas = []
            for ec in range(4):
                def ev(ps, bi, gk, ec=ec):
                    t = ptmp[cnt["ev"] % 2]
                    tk = "ptmp%d" % (cnt["ev"] % 2)
                    cnt["ev"] += 1
                    ACT("activation", [gk], ["junkf"], out=junkf, in_=ps, func=AF.Silu)
                    DVE("tensor_scalar", ["junkf", "convp"], [tk], out=t, in0=junkf, scalar1=convp[:, ec, 34:35], scalar2=None, op0=ALU.mult)
                    tb, tbk = next_tb()
                    for j in range(4):
                        PE("transpose", [tk, "identb"], [tbk], out=tb[:, j, :], in_=t[:, j * 128:(j + 1) * 128], identity=identb[:])
                    DVE("tensor_copy", [tbk], ["ngtok"], out=ngtok[:, bi * 4:(bi + 1) * 4, ec * 128:(ec + 1) * 128], in_=tb[:, 0:4, :])
                d, tk_ = proj_tasks(w_in, OFF_GG + h * 512 + ec * 128, 128, MAINB, ev)
                gdmas.append(d)
                gtasks.extend(tk_)
            gla_prep_all(h, True, v_tasks(h))
            gdmas[0]()
            src_, dst_ = dec, cpA
            srck, dstk = "dec", "cpA"
            for sft in (1, 2, 4, 8):
                DVE("tensor_copy", [srck], [dstk], out=dst_[:, :, 0:sft], in_=src_[:, :, 0:sft])
                DVE("tensor_tensor", [srck], [dstk], out=dst_[:, :, sft:16], in0=src_[:, :, sft:16], in1=src_[:, :, 0:16 - sft], op=ALU.mult)
                if dst_ is cpA:
                    src_, dst_, srck, dstk = cpA, cpB, "cpA", "cpB"
                else:
                    src_, dst_, srck, dstk = cpB, cpA, "cpB", "cpA"
            inc, inck = src_, srck
            DVE("tensor_copy", ["carry"], ["PBf"], out=PBf[:, :, 0], in_=carry[:, 2 * h:2 * h + 2])
            for dc in range(2):
                DVE("tensor_scalar", [inck, "carry"], ["PBf"], out=PBf[:, dc, 1:16], in0=inc[:, dc, 0:15], scalar1=carry[:, 2 * h + dc:2 * h + dc + 1], scalar2=None, op0=ALU.mult)
            DVE("tensor_tensor", [inck, "carry"], ["carry"], out=carry[:, 2 * h:2 * h + 2], in0=inc[:, :, 15], in1=carry[:, 2 * h:2 * h + 2], op=ALU.mult)
            for i in range(8):
                for a in range(2):
                    for dc in range(2):
                        n = 2 * i + a
                        ACT("activation", ["qz", "PBf"], ["qg"], out=qg[:, dc, i * 128 + a * 64:i * 128 + (a + 1) * 64], in_=qz[:, i, a, dc, a * 64:(a + 1) * 64],
                            func=AF.Copy, scale=PBf[:, dc, n:n + 1])
            olo = ab(O_YT, 4096).rearrange("p (i e) -> p i e", i=8)
            gparts = flat_parts(gtasks)

            def gp():
                if gparts:
                    gparts.pop(0)()

            def attn(i):
                tc0 = i * 128
                first = True
                for a in range(2):
                    for dc in range(2):
                        PE("matmul", ["kT", "qz"], ["M0"], M[0][:, 0:128], lhsT=kT[:, dc, tc0:tc0 + 128], rhs=qz[:, i, a, dc, :],
                           start=first, stop=(a == 1 and dc == 1))
                        first = False
                DVE("tensor_tensor", ["M0", "cmask"], ["attnb%d" % (i % 2)], out=attnb[i % 2], in0=M[0][:, 0:128], in1=cmask[:], op=ALU.mult)

            attn(0)
            for i in range(8):
                at, atk = attnb[i % 2], "attnb%d" % (i % 2)
                PE("matmul", [atk, "vtok"], ["M2o"], M[2][:, :], lhsT=at, rhs=vtok[:, i, :], start=True, stop=False)
                for dc in range(2):
                    PE("matmul", ["qz", "stb"], ["M2o"], M[2][:, :], lhsT=qz[:, i, 0, dc, :], rhs=stateb[:, h, dc, :], start=False, stop=False)
                state_update(h, i, 0)
                gp()
                if i + 1 < 8:
                    attn(i + 1)
                gp()
                for dc in range(2):
                    PE("matmul", ["qz", "stb"], ["M2o"], M[2][:, :], lhsT=qz[:, i, 1, dc, :], rhs=stateb[:, h, dc, :], start=False, stop=(dc == 1))
                state_update(h, i, 1)
                copy_op("act", olo[:, i, :], M[2][:, :], ["M2o"], ["olo"])
                gp()
                gp()
            while gparts:
                gp()
            s0 = (hm * 4 + h) * HALF
            P.dma("sp", osc[s0:s0 + HALF, :].rearrange("(i p) e -> p i e", p=128), olo, reads=["olo"], writes=["osc"], sem_key="st_olo")
            P.dma("sp", ngsc[s0:s0 + HALF, :].rearrange("(i p) e -> p i e", p=128), ngtok, reads=["ngtok"], writes=["ngsc"], sem_key="st_ng")
            q0 = (hm * 4 + h) * 256
            P.dma("sp", qgsc[q0:q0 + 256, :].rearrange("(d p) t -> p d t", p=128), qg, reads=["qg"], writes=["qgsc"], sem_key="st_qg")

        def exchange():
            sflat = state32[:].rearrange("p h d e -> p (h d e)")
            P.dma("sp", xsrc[0], sflat[:, 0:2048], reads=["st32"], writes=["xsrc0"], sem_key="st_xs0")
            P.dma("sp", xsrc[1], sflat[:, 2048:4096], reads=["st32"], writes=["xsrc1"], sem_key="st_xs1")
            P.dma("sp", xsrc[2], carry[:], reads=["carry"], writes=["xsrc2"], sem_key="st_xs2")
            for i in range(3):
                P.op("pool", "collective_compute", ["xsrc%d" % i], ["xdst%d" % i], "AllGather", ALU.bypass, replica_groups=[[0, 1, 2, 3], [4, 5, 6, 7]],
                     ins=[xsrc[i]], outs=[xdst[i]], cc=True)

        def start_state_tasks():
            Lh = af(6656, 2048)
            wk = ["w0", "t2w", "w2"]

            def weights():
                for i in range(3):
                    P.dma("sp", pq[:, i, :], xdst[2][i * 128:(i + 1) * 128, 0:8], reads=["xdst2"], writes=["pq%d" % i])
                DVE("tensor_scalar", ["pq1", "selt"], ["t1"], out=wts[:, 3, :], in0=pq[:, 1, :], scalar1=selt[:, 1:2], scalar2=selt[:, 5:6], op0=ALU.mult, op1=ALU.add)
                DVE("tensor_scalar", ["pq2", "selt"], ["t2w"], out=wts[:, 1, :], in0=pq[:, 2, :], scalar1=selt[:, 2:3], scalar2=selt[:, 6:7], op0=ALU.mult, op1=ALU.add)
                DVE("tensor_tensor", ["t1", "t2w"], ["w0"], out=wts[:, 0, :], in0=wts[:, 3, :], in1=wts[:, 1, :], op=ALU.mult)
                DVE("tensor_scalar", ["w0", "selt"], ["w0"], out=wts[:, 0, :], in0=wts[:, 0, :], scalar1=selt[:, 0:1], scalar2=None, op0=ALU.mult)
                DVE("tensor_scalar", ["t2w", "selt"], ["t2w"], out=wts[:, 1, :], in0=wts[:, 1, :], scalar1=selt[:, 1:2], scalar2=None, op0=ALU.mult)
                DVE("memset", [], ["w2"], wts[:, 2, :], 1.0)
                DVE("tensor_scalar", ["w2", "selt"], ["w2"], out=wts[:, 2, :], in0=wts[:, 2, :], scalar1=selt[:, 2:3], scalar2=None, op0=ALU.mult)

            def acc(i, half):
                def task():
                    P.dma("sp", Lh, xdst[half][i * 128:(i + 1) * 128, :], reads=["xdst%d" % half], writes=["Lh"])
                    for q4 in range(4):
                        hd = half * 4 + q4
                        sv = state32[:, hd // 2, hd % 2, :]
                        if i == 0:
                            DVE("tensor_scalar", ["Lh", wk[i]], ["st32"], out=sv, in0=Lh[:, q4 * 512:(q4 + 1) * 512], scalar1=wts[:, i, hd:hd + 1], scalar2=None, op0=ALU.mult)
                        else:
                            DVE("scalar_tensor_tensor", ["Lh", wk[i], "st32"], ["st32"], out=sv, in0=Lh[:, q4 * 512:(q4 + 1) * 512], scalar=wts[:, i, hd:hd + 1], in1=sv,
                                op0=ALU.mult, op1=ALU.add)
                return task

            def casts():
                for h in range(4):
                    for dc in range(2):
                        copy_op("act" if dc else "dve", stateb[:, h, dc, :], state32[:, h, dc, :], ["st32"], ["stb"])
            return [weights] + [acc(i, half) for i in range(3) for half in range(2)] + [casts]

        EB = 10240

        def gla_epilogue_tasks(heads, n_of=1, of_base=None):
            olo = ab(2 * EB, 4096).rearrange("p (i e) -> p i e", i=8)
            ng = ab(2 * EB + 4096, 4096).rearrange("p (i e) -> p i e", i=8)
            qg2 = ab(2 * EB + 8192, 2048).rearrange("p (d t) -> p d t", d=2)
            yT2 = ab(2 * EB + 10240, 4096).rearrange("p (e t) -> p e t", e=4)
            ytk2 = [ab(2 * EB + 14336 + 512 * i, 512) for i in range(2)]
            ofs = [af(EB + 7680, 512)] if n_of == 1 else [af(of_base + 512 * i, 512) for i in range(n_of)]
            jf = af(9728, 512)
            steps = [(h, hm, i) for (h, hm) in heads for i in range(8)]

            def head_loads(k):
                h, hm, i = steps[k]
                s0 = (hm * 4 + h) * HALF
                q0 = (hm * 4 + h) * 256
                P.dma("sp", qg2, qgsc[q0:q0 + 256, :].rearrange("(d p) t -> p d t", p=128), reads=["qgsc"], writes=["qg2"])
                P.dma("sp", olo, osc[s0:s0 + HALF, :].rearrange("(i p) e -> p i e", p=128), reads=["osc"], writes=["olo2"])
                P.dma("sp", ng, ngsc[s0:s0 + HALF, :].rearrange("(i p) e -> p i e", p=128), reads=["ngsc"], writes=["ng2"])

            def front(k):
                h, hm, i = steps[k]
                if i == 0:
                    if k == 0:
                        head_loads(0)
                    DVE("memset", [], ["ss_%d" % c for c in range(8)], ss[:, 0:8], 0.0)
                tc0 = i * 128
                mb, mbk = (M[2], "M2o") if k % 2 == 0 else (M[3], "M3")
                for dc in range(2):
                    PE("matmul", ["qg2", "stb"], [mbk], mb[:, :], lhsT=qg2[:, dc, tc0:tc0 + 128], rhs=stateb[:, h, dc, :], start=(dc == 0), stop=(dc == 1))
                of_, ofk = ofs[k % len(ofs)], "of%d" % (k % len(ofs))
                DVE("tensor_tensor", [mbk, "olo2"], [ofk], out=of_, in0=mb[:, :], in1=olo[:, i, :], op=ALU.add)
                sk, rk = "ss_%d" % i, "rs_%d" % i
                ACT("activation", [ofk, sk], ["junk", sk], out=jf, in_=of_, func=AF.Square, accum_out=ss[:, i:i + 1])
                ACT("activation", [sk, "cst"], [rk], out=ss[:, 8 + i:9 + i], in_=ss[:, i:i + 1], func=AF.Sqrt, bias=cst[:, 1:2], scale=1.0 / 512)
                DVE("reciprocal", [rk], [rk], out=ss[:, 8 + i:9 + i], in_=ss[:, 8 + i:9 + i])
                yt, ytk = ytk2[k % 2], "ytok%d" % (k % 2)
                DVE("scalar_tensor_tensor", [ofk, rk, "ng2"], [ytk], out=yt, in0=of_, scalar=ss[:, 8 + i:9 + i], in1=ng[:, i, :], op0=ALU.mult, op1=ALU.mult)

            def back(k):
                h, hm, i = steps[k]
                tc0 = i * 128
                yt, ytk = ytk2[k % 2], "ytok%d" % (k % 2)
                tb, tbk = next_tb()
                for j in range(4):
                    PE("transpose", [ytk, "identb"], [tbk], out=tb[:, j, :], in_=yt[:, j * 128:(j + 1) * 128], identity=identb[:])
                copy_op("act", yT2[:, :, tc0:tc0 + 128], tb[:, 0:4, :], [tbk], ["yT2"])
                if i == 7:
                    mixs = mixs_all[hm * D:(hm + 1) * D, :]
                    r0 = 1024 + h * 512
                    P.dma("sp", mixs[r0:r0 + 512, :].rearrange("(e p) t -> p e t", p=128), yT2, reads=["yT2"], writes=["mixs%d" % hm], sem_key="st_yT")

            def mk(k):
                def task():
                    if k < len(steps):
                        front(k)
                    if k > 0:
                        back(k - 1)
                    if k + 1 < len(steps) and steps[k + 1][2] == 0:
                        head_loads(k + 1)
                return task
            return [mk(k) for k in range(len(steps) + 1)]

        def mem_bufs(h):
            o = 2048 * 3 * (h % 2)
            mqT = ab(o, 2048).rearrange("p (d t) -> p d t", d=2)
            sgT = ab(o + 2048, 2048).rearrange("p (d t) -> p d t", d=2)
            ymT = ab(o + 4096, 2048).rearrange("p (d t) -> p d t", d=2)
            return mqT, sgT, ymT, "%d" % (h % 2)

        def mem_proj_tasks(h):
            mqT, sgT, ymT, sfx = mem_bufs(h)
            out_ = []
            for dc in range(2):
                def ev(ps, bi, gk, dc=dc):
                    copy_op(evac_eng(), mqT[:, dc, bi * 512:(bi + 1) * 512], ps, [gk], ["mqT" + sfx])
                out_.extend(proj_tasks(w_in, OFF_MQ + h * 256 + dc * 128, 128, MAINB, ev)[1])
            for dc in range(2):
                def ev(ps, bi, gk, dc=dc):
                    ACT("activation", [gk], ["sgT" + sfx], out=sgT[:, dc, bi * 512:(bi + 1) * 512], in_=ps, func=AF.Silu)
                out_.extend(proj_tasks(w_in, OFF_MG + h * 256 + dc * 128, 128, MAINB, ev)[1])
            return out_

        def mem_head(h, hm, tasks=(), extra=(), n_extra=0):
            tasks = list(tasks)
            extra = list(extra)
            mixs = mixs_all[hm * D:(hm + 1) * D, :]
            mqT, sgT, ymT, sfx = mem_bufs(h)
            mqk, sgk, ymk = "mqT" + sfx, "sgT" + sfx, "ymT" + sfx
            Pn = [ab(12288 + 256 * i, 256) for i in range(2)]
            PTs = [ab(12800 + 256 * i, 256).rearrange("p (m t) -> p m t", m=2) for i in range(2)]
            DVE("memset", [], ["sm_%d" % c for c in range(8)], sm[:], 0.0)

            def stage_a(i):
                tc0 = i * 128
                for dc in range(2):
                    PE("matmul", [mqk, "mkT"], ["M0"], M[0][:, 0:256], lhsT=mqT[:, dc, tc0:tc0 + 128], rhs=mkT[:, h * 2 + dc, :], start=(dc == 0), stop=(dc == 1))
                lk = "lst%d" % (i % 2)
                lc = 4 * (i % 2)
                DVE("reduce_max", ["M0"], [lk], out=lst[:, lc:lc + 1], in_=M[0][:, 0:256], axis=AX.X)
                DVE("tensor_scalar", [lk], [lk], out=lst[:, lc + 1:lc + 2], in0=lst[:, lc:lc + 1], scalar1=-0.0625, scalar2=None, op0=ALU.mult)
                pn = Pn[i % 2]
                pk = "Pn%d" % (i % 2)
                ACT("activation", ["M0", lk, "sm_%d" % i], [pk, "sm_%d" % i], out=pn, in_=M[0][:, 0:256], func=AF.Exp, bias=lst[:, lc + 1:lc + 2], scale=0.0625, accum_out=sm[:, i:i + 1])
                DVE("reciprocal", ["sm_%d" % i], [lk], out=lst[:, lc + 2:lc + 3], in_=sm[:, i:i + 1])
                DVE("tensor_scalar", [pk, lk], [pk], out=pn, in0=pn, scalar1=lst[:, lc + 2:lc + 3], scalar2=None, op0=ALU.mult)

            def stage_b(i):
                tc0 = i * 128
                pn = Pn[i % 2]
                pk = "Pn%d" % (i % 2)
                tb, tbk = next_tb()
                for mc in range(2):
                    PE("transpose", [pk, "identb"], [tbk], out=tb[:, mc, :], in_=pn[:, mc * 128:(mc + 1) * 128], identity=identb[:])
                pt = PTs[i % 2]
                ptk = "PTs%d" % (i % 2)
                copy_op("act", pt, tb[:, 0:2, :], [tbk], [ptk])
                for dvc in range(2):
                    for mc in range(2):
                        PE("matmul", ["mv", ptk], ["M1"], M[1][:, dvc * 128:(dvc + 1) * 128], lhsT=mv[:, mc, h * 256 + dvc * 128:h * 256 + (dvc + 1) * 128],
                                                                    rhs=pt[:, mc, :], start=(mc == 0), stop=(mc == 1))
                DVE("tensor_tensor", ["M1", sgk], [ymk], out=ymT[:, :, tc0:tc0 + 128], in0=M[1][:, 0:256].rearrange("p (d t) -> p d t", d=2), in1=sgT[:, :, tc0:tc0 + 128], op=ALU.mult)

            mparts = flat_parts(tasks)

            def mp():
                if mparts:
                    mparts.pop(0)()
            stage_a(0)
            for i in range(8):
                mp()
                if i + 1 < 8:
                    stage_a(i + 1)
                mp()
                mp()
                stage_b(i)
                mp()
                for _ in range(n_extra):
                    if extra:
                        extra.pop(0)()
            while mparts:
                mp()
            for t in extra:
                t()
            r0 = 3072 + h * 256
            P.dma("sp", mixs[r0:r0 + 256, :].rearrange("(e p) t -> p e t", p=128), ymT, reads=[ymk], writes=["mixs"], sem_key="st_ymT" + sfx)

        gbc = state32[:].rearrange("p h d e -> p (h d e)")
        bbc2 = [wbuf[i][:].rearrange("p a b -> p (a b)").bitcast(F32) for i in range(2)]

        def load_ln_params():
            P.dma("sp", gbc, lngb[0:1, :].partition_broadcast(128), reads=["stb"], writes=["st32"], sem_key="ld_gbc")
            for i in range(2):
                P.dma("sp", bbc2[i], lngb[1:2, i * 2048:(i + 1) * 2048].partition_broadcast(128), writes=["wbuf%d_%d" % (i, g) for g in range(4)], sem_key="ld_bbc%d" % i)

        def load_mixT(hm, groups):
            mixT = xT[:, :, XC:XW]
            mixs = mixs_all[hm * D:(hm + 1) * D, :]
            for g in groups:
                P.dma("sp", mixT[:, g * 8:(g + 1) * 8, :], mixs[g * 1024:(g + 1) * 1024, :].rearrange("(kc p) t -> p kc t", p=128),
                      reads=["mixs%d" % hm, "mixs"], writes=["xT", "xT_%d" % g], sem_key="ld_mix%d" % g)

        def out1(hm, tasks=(), per_nb=0, groups=(0, 1, 2, 3)):
            tasks = list(tasks)
            row0 = XC + hm * HALF
            orow0 = hm * HALF
            s1, s2 = s1h[hm], s2h[hm]
            s1k, s2k = "s1_%d" % hm, "s2_%d" % hm
            mixT = xT[:, :, XC:XW]
            mixs = mixs_all[hm * D:(hm + 1) * D, :]
            load_mixT(hm, groups)
            wo = [ab(8192 * i, 8192).rearrange("p (k n) -> p k n", k=32) for i in range(2)]
            xr = [af(8192 + 256 * i, 256) for i in range(3)]
            yb = [af(8960 + 256 * i, 256) for i in range(3)]
            jk = af(9728, 256)
            DVE("memset", [], [s1k], s1[:], 0.0)
            DVE("memset", [], [s2k], s2[:], 0.0)
            q = 0
            for nb in range(16):
                w = wo[nb % 2]
                wk = "wo%d" % (nb % 2)
                wv = w_out[:, nb * 256:(nb + 1) * 256].rearrange("(kc p) n -> p kc n", p=128)
                for g4 in range(4):
                    P.dma("pool", w[:, g4 * 8:(g4 + 1) * 8, :], wv[:, g4 * 8:(g4 + 1) * 8, :], writes=["%s_%d" % (wk, g4)])
                for i in range(8):
                    gi = cnt["g"] % 2
                    cnt["g"] += 1
                    gk = "G%d" % gi
                    for kc in range(32):
                        PE("matmul", ["%s_%d" % (wk, kc // 8), "xT_%d" % (kc // 8)], [gk], G[gi][:, 0:256], lhsT=mixT[:, kc, i * 128:(i + 1) * 128], rhs=w[:, kc, :],
                           start=(kc == 0), stop=(kc == 31))
                    x_, xk = xr[q % 3], "xr%d" % (q % 3)
                    y_, yk = yb[q % 3], "yb%d" % (q % 3)
                    for q2 in ([0, 1, 2] if q == 0 else [q + 2]):
                        if q2 < 128:
                            nb2, i2 = q2 // 8, q2 % 8
                            P.dma("act", xr[q2 % 3], xall[row0 + i2 * 128:row0 + (i2 + 1) * 128, nb2 * 256:(nb2 + 1) * 256], writes=["xr%d" % (q2 % 3)])
                    q += 1
                    col = i * 16 + nb
                    DVE("scalar_tensor_tensor", [gk, xk, s1k], [yk, s1k], out=y_, in0=x_, scalar=ALPHA, in1=G[gi][:, 0:256], op0=ALU.mult, op1=ALU.add,
                        accum_out=s1[:, col:col + 1])
                    ACT("activation", [yk, s2k], ["junk", s2k], out=jk, in_=y_, func=AF.Square, accum_out=s2[:, col:col + 1])
                    P.dma("act", out[orow0 + i * 128:orow0 + (i + 1) * 128, nb * 256:(nb + 1) * 256], y_, reads=[yk], writes=["out%d_%d" % (hm, i)],
                          sem_key="st_yb%d" % (q % 3))
                    if tasks and per_nb and (i + 1) % (8 // per_nb) == 0:
                        tasks.pop(0)()
                if tasks and not per_nb and nb % 2 == 1:
                    tasks.pop(0)()
            for t in tasks:
                t()

        def out2_tasks(hm, yf, defer):
            orow0 = hm * HALF
            s1, s2 = s1h[hm], s2h[hm]
            s1k, s2k = "s1_%d" % hm, "s2_%d" % hm
            nbuf = len(yf)
            pend = []

            def store(i):
                y_, yk = yf[i % nbuf], "yf%d" % (i % nbuf)
                P.dma("sp", out[orow0 + i * 128:orow0 + (i + 1) * 128, :], y_, reads=[yk], writes=["fin%d_%d" % (hm, i)], sem_key="st_yf%d" % (i % nbuf))

            def load(i):
                y_, yk = yf[i % nbuf], "yf%d" % (i % nbuf)
                P.dma("sp", y_, out[orow0 + i * 128:orow0 + (i + 1) * 128, :], reads=["out%d_%d" % (hm, i)], writes=[yk])

            def part(i, c):
                def task():
                    y_, yk = yf[i % nbuf], "yf%d" % (i % nbuf)
                    lo = 6 * (i % 2)
                    lk = "lsu%d" % (i % 2)
                    if c == 0:
                        while pend:
                            store(pend.pop(0))
                        if defer:
                            if i == 0:
                                load(0)
                        else:
                            for t_ in ([0, 1, 2] if i == 0 else [i + 2]):
                                if t_ < 8:
                                    load(t_)
                        DVE("reduce_sum", [s1k], [lk], out=lsu[:, lo:lo + 1], in_=s1[:, i * 16:(i + 1) * 16], axis=AX.X)
                        DVE("reduce_sum", [s2k], [lk], out=lsu[:, lo + 1:lo + 2], in_=s2[:, i * 16:(i + 1) * 16], axis=AX.X)
                        DVE("tensor_scalar", [lk], [lk], out=lsu[:, lo + 2:lo + 3], in0=lsu[:, lo:lo + 1], scalar1=1.0 / D, scalar2=None, op0=ALU.mult)
                        DVE("tensor_tensor", [lk], [lk], out=lsu[:, lo + 3:lo + 4], in0=lsu[:, lo + 2:lo + 3], in1=lsu[:, lo + 2:lo + 3], op=ALU.mult)
                        DVE("scalar_tensor_tensor", [lk], [lk], out=lsu[:, lo + 3:lo + 4], in0=lsu[:, lo + 1:lo + 2], scalar=1.0 / D, in1=lsu[:, lo + 3:lo + 4],
                            op0=ALU.mult, op1=ALU.subtract)
                        ACT("activation", [lk, "cst"], [lk], out=lsu[:, lo + 4:lo + 5], in_=lsu[:, lo + 3:lo + 4], func=AF.Sqrt, bias=cst[:, 1:2], scale=1.0)
                        DVE("reciprocal", [lk], [lk], out=lsu[:, lo + 4:lo + 5], in_=lsu[:, lo + 4:lo + 5])
                        DVE("scalar_tensor_tensor", [lk], [lk], out=lsu[:, lo + 5:lo + 6], in0=lsu[:, lo + 2:lo + 3], scalar=-1.0, in1=lsu[:, lo + 4:lo + 5],
                            op0=ALU.mult, op1=ALU.mult)
                    cs = slice(c * 1024, (c + 1) * 1024)
                    ykc = "%s_c%d" % (yk, c)
                    ACT("activation", [yk, lk], [ykc], out=y_[:, cs], in_=y_[:, cs], func=AF.Identity, scale=lsu[:, lo + 4:lo + 5], bias=lsu[:, lo + 5:lo + 6])
                    DVE("tensor_tensor", [ykc, "st32"], [ykc], out=y_[:, cs], in0=y_[:, cs], in1=gbc[:, cs], op=ALU.mult)
                    DVE("tensor_tensor", [ykc, "wbuf%d_0" % (c // 2)], [ykc, yk] if c == 3 else [ykc], out=y_[:, cs], in0=y_[:, cs],
                        in1=bbc2[c // 2][:, (c % 2) * 1024:(c % 2 + 1) * 1024], op=ALU.add)
                    if c == 3:
                        if defer:
                            pend.append(i)
                        else:
                            store(i)
                    if defer and c == 2 and i + 1 < 8:
                        load(i + 1)
                return task

            def flush():
                while pend:
                    store(pend.pop(0))
            return [part(i, c) for i in range(8) for c in range(4)] + [flush]

        for hm in range(n_main):
            P.barrier()
            build_xT(XC + hm * HALF, 8, XC)
            if hm == 0:
                build_xT(0, 1, 0, rows_per=XC)
            P.barrier()
            conv_phase(hm)
            P.barrier()
            lr_proj()
            gla_zero_pads()
            for h in range(4):
                gla_head_main(h, hm)
            P.barrier()
            if hm == n_main - 1:
                exchange()
            for t in mem_proj_tasks(0):
                t()
            if hm < n_main - 1:
                for h in range(4):
                    mem_head(h, hm, mem_proj_tasks(h + 1) if h < 3 else ())
            else:
                ep0 = gla_epilogue_tasks([(h, 0) for h in range(4)], n_of=2, of_base=8704)
                mem_head(0, hm, mem_proj_tasks(1))
                mem_head(1, hm, mem_proj_tasks(2), start_state_tasks(), 1)
                mem_head(2, hm, mem_proj_tasks(3), ep0[:17], 3)
                load_mixT(0, [0])
                mem_head(3, hm, (), ep0[17:], 2)
                load_mixT(0, [1, 2, 3])
        P.barrier()
        load_ln_params()
        out1(0, gla_epilogue_tasks([(h, 1) for h in range(4)]), per_nb=2, groups=())
        P.barrier()
        out1(1, out2_tasks(0, [af(EB, 4096), af(EB + 4096, 4096)], True), per_nb=2)
        P.barrier()
        for t in out2_tasks(1, [af(0, 4096), af(4096, 4096), af(8192, 4096), af(12288, 4096)], False):
            t()
        if dbg:
            pass
        P.barrier()
        P.op("sp", None)
        P.emit(block, st)
        build.n_ops = len(P.ops)
        build.n_sems = P.n_sems
    return nc


def _consts():
    ident = np.eye(128, dtype=np.float32)
    s = np.arange(128)[:, None]
    t = np.arange(128)[None, :]
    same = (s // 64) == (t // 64)
    tri = (same & (s <= t)).astype(np.float32)
    aft = (same & (s > t)).astype(np.float32)
    m2 = np.concatenate([tri, aft], axis=1) * np.float32(-1.0 / 16.0)
    return ident, np.ascontiguousarray(m2.astype(np.float32)), tri


def make_in_maps(x, mem, w_in, conv_dw, conv_dw_b, conv_ln_g, conv_ln_b, gla_w_gate, gla_gate_b, gla_norm_g,
                 w_mem_kv, w_out, ln_g, ln_b):
    f = lambda a: np.ascontiguousarray(np.asarray(a, dtype=np.float32))
    x = f(x); mem = f(mem)
    ident, m2, cm = _consts()
    prm = np.zeros((35, 1024), np.float32)
    prm[0:31] = f(conv_dw)[0]
    prm[31] = f(conv_dw_b)[0]
    prm[32] = f(conv_ln_g)[0]
    prm[33] = f(conv_ln_b)[0]
    prm[34, 0:512] = f(gla_norm_g)[0]
    wga = np.concatenate([f(gla_w_gate)[0], f(gla_gate_b)[0][None, :]], axis=0)
    lngb = np.stack([f(ln_g)[0], f(ln_b)[0]], axis=0)
    w_in0, w_mkv0, w_out0 = f(w_in)[0], f(w_mem_kv)[0], f(w_out)[0]
    maps = []
    for j in range(8):
        b, r = j // 4, j % 4
        xa = np.zeros((XC + NQ, D), np.float32)
        xa[XC:] = x[b, r * NQ:(r + 1) * NQ]
        if r > 0:
            xa[0:XC] = x[b, r * NQ - XC:r * NQ]
        sel = np.zeros((128, 8), np.float32)
        for i in range(4):
            sel[:, i] = 1.0 if i < r else 0.0
            sel[:, 4 + i] = 0.0 if i < r else 1.0
        maps.append({"sel": sel, "xall": xa, "mem": mem[b], "w_in": w_in0, "w_mkv": w_mkv0, "w_out": w_out0, "prm": prm, "wga": f(wga),
                     "lngb": f(lngb), "identd": ident, "m2": m2, "cmaskd": cm})
    return maps


def kernel(**inputs):
    nc = build()
    maps = make_in_maps(**inputs)
    res = run_bass_kernel_spmd(nc, maps, core_ids=list(range(8)))
    outs = [np.asarray(r["out"], dtype=np.float32) for r in res.results]
    y = np.stack([np.concatenate(outs[0:4], axis=0), np.concatenate(outs[4:8], axis=0)], axis=0)
    return y
```

```python
from contextlib import ExitStack
import numpy as np
import concourse.bass as bass
import concourse.mybir as mybir
from concourse.bass_utils import run_bass_kernel_spmd

F32 = mybir.dt.float32
BF16 = mybir.dt.bfloat16
AF = mybir.ActivationFunctionType
ALU = mybir.AluOpType
AX = mybir.AxisListType

EPOCH = 8000
D = 4096
D_IN = 11280
NQ = 2048
HALF = 1024
XC = 32
XW = XC + HALF
OFF_CA, OFF_CB, OFF_CG, OFF_Q, OFF_K, OFF_V, OFF_LR, OFF_GG, OFF_MQ, OFF_MG = 0, 1024, 2048, 3072, 4096, 5120, 7168, 7184, 9232, 10256
ALPHA = 2.0 ** 0.25
LN_EPS = 1e-5


class Prog:
    ENG = ("pe", "act", "dve", "pool", "sp")

    def __init__(self, nc):
        self.nc = nc
        self.ops = []
        self.last_write = {}
        self.readers = {}
        self.last_on = {}
        self.dmas_since_bar = []
        self.bar = None
        self.bar_done = set()

    def barrier(self):
        b = set(self.last_on.values())
        b.update(self.dmas_since_bar)
        self.bar = b
        self.bar_done = set()
        self.dmas_since_bar = []

    def op(self, eng, fn, reads=(), writes=(), *args, dma=False, sem_key=None, cc=False, **kw):
        if cc:
            dma = "cc"
            sem_key = sem_key or ("cc", len(self.ops))
        if isinstance(fn, str):
            fn = (fn, args, kw)
        deps = set()
        for k in reads:
            w = self.last_write.get(k)
            if w is not None:
                deps.add(w)
        for k in writes:
            w = self.last_write.get(k)
            if w is not None:
                deps.add(w)
            r = self.readers.get(k)
            if r:
                deps.update(r.values())
        if self.bar is not None and eng not in self.bar_done:
            deps.update(self.bar)
            self.bar_done.add(eng)
        idx = len(self.ops)
        if dma and sem_key is None:
            sem_key = writes[0]
        self.ops.append([eng, fn, deps, dma, sem_key])
        rk = ("d", idx) if dma else eng
        for k in reads:
            self.readers.setdefault(k, {})[rk] = idx
        for k in writes:
            self.last_write[k] = idx
            self.readers[k] = {}
        if dma:
            self.dmas_since_bar.append(idx)
        else:
            self.last_on[eng] = idx
        return idx

    def dma(self, eng, out, in_, reads=(), writes=(), sem_key=None):
        return self.op(eng, "dma_start", reads, writes, dma=True, sem_key=sem_key, out=out, in_=in_)

    def emit(self, block, stack):
        nc = self.nc
        ops = self.ops
        n = len(ops)
        needed = [False] * n
        seen = {e: {} for e in self.ENG}
        final_deps = [None] * n
        for i, (eng, fn, deps, dma, sk) in enumerate(ops):
            per = {}
            fd = []
            s = seen[eng]
            for d in deps:
                de, _, _, ddma, _ = ops[d]
                if ddma:
                    key = ("d", d)
                    if key in s:
                        continue
                    s[key] = True
                    fd.append(d)
                else:
                    if de == "pe" and eng == "pe" and not dma:
                        continue
                    if per.get(de, -1) < d:
                        per[de] = d
            for de, d in per.items():
                if s.get(de, -1) >= d:
                    continue
                s[de] = d
                fd.append(d)
            final_deps[i] = fd
            for d in fd:
                needed[d] = True
        cnt = {e: 0 for e in self.ENG}
        eng_sems = {e: [] for e in self.ENG}
        dma_sems = {}
        dma_cnt = {}
        token = [None] * n
        for i, (eng, fn, deps, dma, sk) in enumerate(ops):
            if dma:
                if sk not in dma_sems:
                    dma_sems[sk] = stack.enter_context(nc.semaphore("dq%d" % len(dma_sems)))
                    dma_cnt[sk] = 0
                inc = 1 if dma == "cc" else 16
                dma_cnt[sk] += inc
                token[i] = (dma_sems[sk], dma_cnt[sk], inc)
            elif needed[i]:
                c = cnt[eng]
                ep = c // EPOCH
                while len(eng_sems[eng]) <= ep:
                    eng_sems[eng].append(stack.enter_context(nc.semaphore("s_%s%d" % (eng, len(eng_sems[eng])))))
                cnt[eng] = c + 1
                token[i] = (eng_sems[eng][ep], c % EPOCH + 1, 1)
        self.n_sems = len(dma_sems) + sum(len(v) for v in eng_sems.values())
        per_eng = {e: [] for e in self.ENG}
        for i, o in enumerate(ops):
            per_eng[o[0]].append(i)

        def run(engname, e):
            for i in per_eng[engname]:
                fn = ops[i][1]
                for d in final_deps[i]:
                    sem, val, _ = token[d]
                    e.wait_ge(sem, val)
                if fn is not None:
                    ins = getattr(e, fn[0])(*fn[1], **fn[2])
                    if token[i] is not None:
                        ins.then_inc(token[i][0], token[i][2])

        block.tensor(lambda e: run("pe", e))
        block.scalar(lambda e: run("act", e))
        block.vector(lambda e: run("dve", e))
        block.gpsimd(lambda e: run("pool", e))
        block.sync(lambda e: run("sp", e))


def build(n_pre=0, n_main=2, dbg=False):
    nc = bass.Bass("TRN2", target_bir_lowering=False, num_devices=8)
    dram_in = lambda name, shape, dt=F32: nc.dram_tensor(name, list(shape), dt, kind="ExternalInput").ap()
    xall = dram_in("xall", [XC + NQ, D])
    memd = dram_in("mem", [256, D])
    w_in = dram_in("w_in", [D, D_IN])
    w_mkv = dram_in("w_mkv", [D, 2048])
    w_out = dram_in("w_out", [D, D])
    prm = dram_in("prm", [35, 1024])
    wga = dram_in("wga", [17, 1024])
    lngb = dram_in("lngb", [2, D])
    identd = dram_in("identd", [128, 128])
    m2d = dram_in("m2", [128, 256])
    cmaskd = dram_in("cmaskd", [128, 128])
    out = nc.dram_tensor("out", [NQ, D], F32, kind="ExternalOutput").ap()
    mixs_all = nc.dram_tensor("mixs", [2 * D, HALF], BF16).ap()
    osc = nc.dram_tensor("osc", [8 * HALF, 512], BF16).ap()
    ngsc = nc.dram_tensor("ngsc", [8 * HALF, 512], BF16).ap()
    qgsc = nc.dram_tensor("qgsc", [8 * 256, HALF], BF16).ap()
    xsrc = [nc.dram_tensor("xsrc%d" % i, [128, w], F32).ap() for i, w in enumerate((2048, 2048, 64))]
    xdst = [nc.dram_tensor("xdst%d" % i, [4 * 128, w], F32).ap() for i, w in enumerate((2048, 2048, 64))]
    seld = dram_in("sel", [128, 8])
    dbgt = nc.dram_tensor("dbg", [128, 4096], F32, kind="ExternalOutput").ap() if dbg else None

    st = ExitStack()
    with st:
        T = lambda name, shape, dt: st.enter_context(nc.sbuf_tensor(name, list(shape), dt))
        PT = lambda name, shape, dt: st.enter_context(nc.psum_tensor(name, list(shape), dt))
        xT = T("xT", [128, 32, XW], BF16)
        state32 = T("state32", [128, 4, 2, 512], F32)
        stateb = T("stateb", [128, 4, 2, 512], BF16)
        mkT = T("mkT", [128, 8, 256], BF16)
        mv = T("mv", [128, 2, 1024], BF16)
        identf = T("identf", [128, 128], F32)
        identb = T("identb", [128, 128], BF16)
        onesb = T("onesb", [128, 128], BF16)
        M2 = T("M2", [128, 256], F32)
        cmask = T("cmask", [128, 128], F32)
        wgs = T("wgs", [32, 1024], F32)
        lrT = T("lrT", [32, HALF], F32)
        convp = T("convp", [128, 8, 35], F32)
        prms = T("prms", [35, 1024], F32)
        cst = T("cst", [128, 4], F32)
        tailT = T("tailT", [128, 8, 32], BF16)
        ss = T("ss", [128, 16], F32)
        sm = T("sm", [128, 8], F32)
        s1h = [T("s1_%d" % i, [128, 128], F32) for i in range(2)]
        s2h = [T("s2_%d" % i, [128, 128], F32) for i in range(2)]
        lst = T("lst", [128, 8], F32)
        lsu = T("lsu", [128, 12], F32)
        carry = T("carry", [128, 64], F32)
        cpA = T("cpA", [128, 2, 16], F32)
        cpB = T("cpB", [128, 2, 16], F32)
        PBf = T("PBf", [128, 2, 16], F32)
        selt = T("selt", [128, 8], F32)
        pq = T("pq", [128, 3, 8], F32)
        wts = T("wts", [128, 4, 8], F32)
        wbuf = [T("wbuf%d" % i, [128, 32, 128], BF16) for i in range(2)]
        NAR = 18432
        arena = T("arena", [128, NAR], F32)
        G = [PT("G%d" % i, [128, 512], F32) for i in range(2)]
        M = [PT("PM%d" % i, [128, 512], F32) for i in range(4)]
        TB = [PT("TB%d" % i, [128, 8, 128], BF16) for i in range(2)]
        block = st.enter_context(nc.Block())
        P = Prog(nc)
        cnt = {"w": 0, "g": 0, "tb": 0, "ev": 0}

        def af(o, n):
            assert o + n <= NAR, (o, n)
            return arena[:, o:o + n]

        def ab(o, n):
            assert o % 2 == 0 and n % 2 == 0 and (o + n) // 2 <= NAR, (o, n)
            return arena[:, o // 2:(o + n) // 2].bitcast(BF16)

        ACT = lambda fn, r, w, *a, **k: P.op("act", fn, r, w, *a, **k)
        DVE = lambda fn, r, w, *a, **k: P.op("dve", fn, r, w, *a, **k)
        PE = lambda fn, r, w, *a, **k: P.op("pe", fn, r, w, *a, **k)
        POOL = lambda fn, r, w, *a, **k: P.op("pool", fn, r, w, *a, **k)

        def next_tb():
            i = cnt["tb"] % 2
            cnt["tb"] += 1
            return TB[i], "TB%d" % i

        def evac_eng():
            cnt["ev"] += 1
            return "dve" if cnt["ev"] % 2 else "act"

        def copy_op(eng, o, i, r, w):
            if eng == "act":
                P.op("act", "activation", r, w, out=o, in_=i, func=AF.Copy)
            else:
                P.op(eng, "tensor_copy", r, w, out=o, in_=i)

        P.dma("sp", identf[:], identd, writes=["identf"])
        P.dma("sp", M2[:], m2d, writes=["M2c"])
        P.dma("sp", cmask[:], cmaskd, writes=["cmask"])
        P.dma("sp", wgs[0:17, :], wga, writes=["wgs"])
        P.dma("sp", prms[:], prm, writes=["prms"])
        DVE("tensor_copy", ["identf"], ["identb"], out=identb[:], in_=identf[:])
        DVE("memset", [], ["onesb"], onesb[:], 1.0)
        DVE("memset", [], ["lrT"], lrT[:], 1.0)
        DVE("memset", [], ["cst"], cst[:, 0:1], 1.0)
        DVE("memset", [], ["cst"], cst[:, 1:2], LN_EPS)
        DVE("memset", [], ["cst"], cst[:, 2:3], 0.0)
        DVE("memset", [], ["st32"], state32[:], 0.0)
        DVE("memset", [], ["stb"], stateb[:], 0.0)
        DVE("memset", [], ["tailT"], tailT[:], 0.0)
        DVE("memset", [], ["carry"], carry[:], 1.0)
        P.dma("sp", selt[:], seld, writes=["selt"])
        for cc in range(8):
            PE("matmul", ["prms", "identf"], ["M0"], M[0][:, cc * 35:(cc + 1) * 35], lhsT=prms[:, cc * 128:(cc + 1) * 128], rhs=identf[0:35, 0:35], start=True, stop=True)
        DVE("tensor_copy", ["M0"], ["convp"], out=convp[:], in_=M[0][:, 0:280].rearrange("p (a b) -> p a b", a=8))

        def build_xT(src_rows, n_tiles, col0, rows_per=128):
            xs2 = [af(0, 4096), af(4096, 4096)]
            xb2 = [ab(16384, 4096), ab(20480, 4096)]
            for i in range(n_tiles):
                xs, xb = xs2[i % 2], xb2[i % 2]
                xsk, xbk = "xs%d" % (i % 2), "xb%d_" % (i % 2)
                r0 = src_rows + i * rows_per
                P.dma("sp", xs[0:rows_per, :], xall[r0:r0 + rows_per, :] if src_rows >= 0 else memd[i * rows_per:(i + 1) * rows_per, :],
                      writes=[xsk])
                ACT("activation", [xsk], [xbk + "0"], out=xb[0:rows_per, 0:2048], in_=xs[0:rows_per, 0:2048], func=AF.Copy)
                DVE("tensor_copy", [xsk], [xbk + "1"], out=xb[0:rows_per, 2048:4096], in_=xs[0:rows_per, 2048:4096])
                for g in range(4):
                    tb, tbk = next_tb()
                    for j in range(8):
                        kc = g * 8 + j
                        PE("transpose", [xbk + ("0" if kc < 16 else "1"), "identb"], [tbk], out=tb[:, j, 0:rows_per], in_=xb[0:rows_per, kc * 128:(kc + 1) * 128],
                                                                   identity=identb[0:rows_per, 0:rows_per])
                    c0 = col0 + i * rows_per
                    copy_op(evac_eng(), xT[:, g * 8:(g + 1) * 8, c0:c0 + rows_per], tb[:, :, 0:rows_per], [tbk], ["xT"])

        def proj_tasks(wsrc, col0, width, blocks, evac):
            stt = {}

            def dma_fn():
                if "wb" in stt:
                    return
                slot = cnt["w"] % 2
                cnt["w"] += 1
                stt["wb"] = wbuf[slot]
                stt["wk"] = "wbuf%d" % slot
                wv = wsrc[:, col0:col0 + width].rearrange("(kc p) n -> p kc n", p=128)
                for g4 in range(4):
                    P.dma("pool", stt["wb"][:, g4 * 8:(g4 + 1) * 8, 0:width], wv[:, g4 * 8:(g4 + 1) * 8, :], writes=["%s_%d" % (stt["wk"], g4)])

            def mk(bi, c0, n):
                loc = {}

                def part(p):
                    def f():
                        if p == 0:
                            dma_fn()
                            loc["gi"] = cnt["g"] % 2
                            cnt["g"] += 1
                        wb, wk = stt["wb"], stt["wk"]
                        gi = loc["gi"]
                        gk = "G%d" % gi
                        for kc in range(8 * p, 8 * p + 8):
                            PE("matmul", ["%s_%d" % (wk, kc // 8), "xT"], [gk], G[gi][0:width, 0:n], lhsT=wb[:, kc, 0:width], rhs=xT[:, kc, c0:c0 + n],
                               start=(kc == 0), stop=(kc == 31))
                        if p == 3:
                            evac(G[gi][0:width, 0:n], bi, gk)
                    return f
                parts = [part(p) for p in range(4)]

                def task():
                    for f in parts:
                        f()
                task.parts = parts
                return task
            return dma_fn, [mk(bi, c0, n) for bi, (c0, n) in enumerate(blocks)]

        def flat_parts(tasks):
            return [p for t in tasks for p in t.parts]

        def proj(wsrc, col0, width, blocks, evac):
            d, tasks = proj_tasks(wsrc, col0, width, blocks, evac)
            for t in tasks:
                t()

        MAINB = [(XC, 512), (XC + 512, 512)]

        build_xT(-1, 2, XC)
        for c in range(8):
            def ev(ps, bi, gk, c=c):
                DVE("tensor_copy", [gk], ["mkT"], out=mkT[:, c, :], in_=ps)
            proj(w_mkv, c * 128, 128, [(XC, 256)], ev)
        mvtmp = ab(26000, 256)
        for c in range(8):
            def ev(ps, bi, gk, c=c):
                DVE("tensor_copy", [gk], ["mvtmp"], out=mvtmp, in_=ps)
                tb, tbk = next_tb()
                for mc in range(2):
                    PE("transpose", ["mvtmp", "identb"], [tbk], out=tb[:, mc, :], in_=mvtmp[:, mc * 128:(mc + 1) * 128], identity=identb[:])
                DVE("tensor_copy", [tbk], ["mv"], out=mv[:, :, c * 128:(c + 1) * 128], in_=tb[:, 0:2, :])
            proj(w_mkv, 1024 + c * 128, 128, [(XC, 256)], ev)
        P.barrier()

        O_QZ, O_KT, O_KS, O_KSZ, O_VT, O_NG, O_YT, O_QT = 0, 4096, 6144, 8192, 12288, 16384, 20480, 24576
        O_TMP = 26624
        F_SP = 14464 + 64
        qz = ab(O_QZ, 4096).rearrange("p (i a d c) -> p i a d c", i=8, a=2, d=2)
        kT = ab(O_KT, 2048).rearrange("p (d t) -> p d t", d=2)
        ksT = ab(O_KS, 2048).rearrange("p (d t) -> p d t", d=2)
        ksz = ab(O_KSZ, 4096).rearrange("p (i a f) -> p i a f", i=8, a=2)
        vtok = ab(O_VT, 4096).rearrange("p (i e) -> p i e", i=8)
        ngtok = ab(O_NG, 4096).rearrange("p (i e) -> p i e", i=8)
        yT = ab(O_YT, 4096).rearrange("p (e t) -> p e t", e=4)
        qT = ab(O_QT, 2048).rearrange("p (d t) -> p d t", d=2)
        ptmp = [ab(O_TMP + 512 * i, 512) for i in range(2)]
        attnb = [ab(O_TMP + 1024 + 128 * i, 128) for i in range(2)]
        ytok = [ab(O_TMP + 1280 + 512 * i, 512) for i in range(2)]
        spt = af(F_SP, 256)
        tmpf = af(F_SP + 256, 256)
        Ef = [af(F_SP + 512 + 768 * i, 768).rearrange("p (a d t) -> p a d t", a=3, d=2) for i in range(2)]
        dec = af(F_SP + 2048, 32).rearrange("p (d n) -> p d n", d=2)
        junkf = af(F_SP + 2080, 512)

        def gla_zero_pads():
            DVE("memset", [], ["qz"], ab(O_QZ, 4096), 0.0)
            DVE("memset", [], ["ksz"], ab(O_KSZ, 4096), 0.0)

        def lr_proj():
            def ev(ps, bi, gk):
                DVE("tensor_copy", [gk], ["lrT"], out=lrT[0:16, bi * 512:(bi + 1) * 512], in_=ps)
            proj(w_in, OFF_LR, 16, MAINB, ev)

        def k_proj(h):
            for dc in range(2):
                def ev(ps, bi, gk, dc=dc):
                    copy_op(evac_eng(), kT[:, dc, bi * 512:(bi + 1) * 512], ps, [gk], ["kT"])
                proj(w_in, OFF_K + h * 256 + dc * 128, 128, MAINB, ev)

        def v_tasks(h):
            out_ = []
            for ec in range(4):
                def ev(ps, bi, gk, ec=ec):
                    t = ptmp[cnt["ev"] % 2]
                    tk = "ptmp%d" % (cnt["ev"] % 2)
                    copy_op(evac_eng(), t, ps, [gk], [tk])
                    tb, tbk = next_tb()
                    for j in range(4):
                        PE("transpose", [tk, "identb"], [tbk], out=tb[:, j, :], in_=t[:, j * 128:(j + 1) * 128], identity=identb[:])
                    DVE("tensor_copy", [tbk], ["vtok"], out=vtok[:, bi * 4:(bi + 1) * 4, ec * 128:(ec + 1) * 128], in_=tb[:, 0:4, :])
                d, tk_ = proj_tasks(w_in, OFF_V + h * 512 + ec * 128, 128, MAINB, ev)
                out_.extend(tk_)
            return out_

        def kv_proj(h, main):
            k_proj(h)
            for t in v_tasks(h):
                t()

        def gla_prep_s1a(h, i, main):
            tc0 = i * 128
            PE("matmul", ["lrT", "wgs"], ["M0"], M[0][:, 0:256], lhsT=lrT[0:17, tc0:tc0 + 128], rhs=wgs[0:17, h * 256:(h + 1) * 256], start=True, stop=True)
            ACT("activation", ["M0"], ["tmpf"], out=tmpf, in_=M[0][:, 0:256], func=AF.Exp, scale=-1.0)
            ACT("activation", ["tmpf", "cst"], ["spt"], out=spt, in_=tmpf, func=AF.Ln, bias=cst[:, 0:1], scale=1.0)

        def gla_prep_s1b(h, i, main):
            mb, mbk = (M[1], "M1") if i % 2 == 0 else (M[3], "M3")
            for fc in range(2):
                PE("matmul", ["spt", "M2c"], [mbk], mb[:, fc * 256:(fc + 1) * 256], lhsT=spt[:, fc * 128:(fc + 1) * 128], rhs=M2[:], start=True, stop=True)

        def gla_prep_s2a(h, i, main):
            tc0 = i * 128
            mb, mbk = (M[1], "M1") if i % 2 == 0 else (M[3], "M3")
            E = Ef[i % 2]
            ek = "Ef%d" % (i % 2)
            m1v = mb[:, :].rearrange("p (d t) -> p d t", d=2)
            ACT("activation", [mbk], [ek], out=E[:, 0], in_=m1v[:, :, 0:128], func=AF.Exp)
            ACT("activation", [mbk], [ek], out=E[:, 2], in_=m1v[:, :, 128:256], func=AF.Exp)
            if main:
                ACT("activation", [mbk], [ek], out=E[:, 1], in_=m1v[:, :, 0:128], func=AF.Exp, scale=-1.0)
            DVE("tensor_copy", [ek], ["dec"], out=dec[:, :, 2 * i:2 * i + 2], in_=E[:, 0, :, 63:128:64])
            DVE("tensor_tensor", [ek, "kT"], ["ksT"], out=ksT[:, :, tc0:tc0 + 128], in0=kT[:, :, tc0:tc0 + 128], in1=E[:, 2], op=ALU.mult)
            if main:
                DVE("tensor_tensor", [ek, "kT"], ["kT"], out=kT[:, :, tc0:tc0 + 128], in0=kT[:, :, tc0:tc0 + 128], in1=E[:, 1], op=ALU.mult)
                for a in range(2):
                    DVE("scalar_tensor_tensor", [ek, "qT"], ["qz"], out=qz[:, i, a, :, a * 64:(a + 1) * 64], in0=qT[:, :, tc0 + a * 64:tc0 + (a + 1) * 64],
                        scalar=0.0625, in1=E[:, 0, :, a * 64:(a + 1) * 64], op0=ALU.mult, op1=ALU.mult)

        def gla_prep_s2b(h, i, main):
            tc0 = i * 128
            tb, tbk = next_tb()
            for fc in range(2):
                PE("transpose", ["ksT", "identb"], [tbk], out=tb[:, fc, :], in_=ksT[:, fc, tc0:tc0 + 128], identity=identb[:])
            tbv = tb[:, 0:2, :]
            DVE("tensor_copy", [tbk], ["ksz"], out=ksz[0:64, i, 0, :].rearrange("p (d f) -> p d f", d=2), in_=tbv[0:64])
            ACT("activation", [tbk], ["ksz"], out=ksz[64:128, i, 1, :].rearrange("p (d f) -> p d f", d=2), in_=tbv[64:128], func=AF.Copy)

        def gla_prep_all(h, main, tasks=()):
            parts = flat_parts(tasks)

            def pp():
                if parts:
                    parts.pop(0)()
            gla_prep_s1a(h, 0, main)
            gla_prep_s1b(h, 0, main)
            for i in range(8):
                if i + 1 < 8:
                    gla_prep_s1a(h, i + 1, main)
                pp()
                if i + 1 < 8:
                    gla_prep_s1b(h, i + 1, main)
                gla_prep_s2a(h, i, main)
                pp()
                pp()
                gla_prep_s2b(h, i, main)
                pp()
            while parts:
                pp()

        def state_update(h, i, a):
            n = 2 * i + a
            for dc in range(2):
                mb, mbk = (M[3], "M3") if dc == 0 else (M[1], "M1")
                PE("matmul", ["ksz", "vtok"], [mbk], mb[:, :], lhsT=ksz[:, i, a, dc * 128:(dc + 1) * 128], rhs=vtok[:, i, :], start=True, stop=True)
                DVE("scalar_tensor_tensor", [mbk, "dec", "st32"], ["st32"], out=state32[:, h, dc, :], in0=state32[:, h, dc, :], scalar=dec[:, dc, n:n + 1],
                                                           in1=mb[:, :], op0=ALU.mult, op1=ALU.add)
                ACT("activation", ["st32"], ["stb"], out=stateb[:, h, dc, :], in_=state32[:, h, dc, :], func=AF.Copy)

        for hp in range(n_pre):
            P.barrier()
            build_xT((6 - n_pre + hp) * HALF, 8, XC)
            P.barrier()
            lr_proj()
            gla_zero_pads()
            for h in range(4):
                kv_proj(h, False)
                gla_prep_all(h, False)
                for i in range(8):
                    for a in range(2):
                        state_update(h, i, a)
                P.barrier()

        def conv_phase(hm):
            mixs = mixs_all[hm * D:(hm + 1) * D, :]
            hT = ab(0, 8 * XW).rearrange("p (c t) -> p c t", c=8)
            diag = ab(8448, 31 * 128).rearrange("p (j c) -> p j c", j=31)
            sig = ab(12416, XW)
            sqt = [ab(13472 + 512 * i, 512) for i in range(2)]
            gt = [ab(14496 + 512 * i, 512) for i in range(2)]
            at_ = [ab(15520 + 512 * i, 512) for i in range(2)]
            ycv2 = [ab(16544, 1024), ab(10496, 1024)]
            rstd = af(8800, 1024)
            nmr = af(9824, 1024)
            mu = af(10848, 512)
            t2 = af(11360, 512)
            tf = [af(11872 + 512 * i, 512) for i in range(2)]
            blocks = MAINB + ([(0, XC)] if hm == 0 else [])
            accs = [af(4224, 1024), af(12896, 1024)]

            def conv_ops(cc):
                hk = "hT%d" % cc
                acc, ak = accs[cc % 2], "acc%d" % (cc % 2)
                ops = []
                for j in range(31):
                    def tap(j=j):
                        src = hT[:, cc, 2 + j:2 + j + HALF]
                        if j == 0:
                            DVE("tensor_scalar", [hk, "convp"], [ak], out=acc, in0=src, scalar1=convp[:, cc, 0:1], scalar2=None, op0=ALU.mult)
                        else:
                            DVE("scalar_tensor_tensor", [hk, "convp", ak], [ak], out=acc, in0=src, scalar=convp[:, cc, j:j + 1], in1=acc, op0=ALU.mult, op1=ALU.add)
                    ops.append(tap)

                def fin():
                    for b in range(2):
                        cv = hT[:, cc, XC + 512 * b:XC + 512 * (b + 1)]
                        ACT("activation", [ak, "convp"], [hk], out=cv, in_=acc[:, 512 * b:512 * (b + 1)], func=AF.Identity, bias=convp[:, cc, 31:32], scale=1.0)
                        ACT("activation", [hk], ["sqt%d" % b], out=sqt[b], in_=cv, func=AF.Square)
                        PE("matmul", [hk, "onesb"], ["M%d" % b], M[b][:, :], lhsT=onesb[:], rhs=cv, start=(cc == 0), stop=(cc == 7))
                        PE("matmul", ["sqt%d" % b, "onesb"], ["M%d" % (2 + b)], M[2 + b][:, :], lhsT=onesb[:], rhs=sqt[b], start=(cc == 0), stop=(cc == 7))
                ops.append(fin)
                return ops

            pending = []
            per_task = 6 if hm == 0 else 8
            for cc in range(8):
                def ev_b(ps, bi, gk):
                    c0, n = blocks[bi]
                    ACT("activation", [gk], ["sig"], out=sig[:, c0:c0 + n], in_=ps, func=AF.Sigmoid)

                def ev_a(ps, bi, gk, cc=cc):
                    c0, n = blocks[bi]
                    DVE("tensor_tensor", [gk, "sig"], ["hT%d" % cc], out=hT[:, cc, c0:c0 + n], in0=ps, in1=sig[:, c0:c0 + n], op=ALU.mult)
                tasks_ = proj_tasks(w_in, OFF_CB + cc * 128, 128, blocks, ev_b)[1] + proj_tasks(w_in, OFF_CA + cc * 128, 128, blocks, ev_a)[1]
                for t in tasks_:
                    t()
                    for _ in range(per_task):
                        if pending:
                            pending.pop(0)()
                while pending:
                    pending.pop(0)()
                hk = "hT%d" % cc
                if hm == 1:
                    DVE("tensor_copy", ["tailT"], [hk], out=hT[:, cc, 0:XC], in_=tailT[:, cc, :])
                DVE("tensor_copy", [hk], ["tailT"], out=tailT[:, cc, :], in_=hT[:, cc, HALF:XW])
                pending = conv_ops(cc)
            while pending:
                pending.pop(0)()
            for b in range(2):
                sl = slice(512 * b, 512 * (b + 1))
                DVE("tensor_scalar", ["M%d" % b], ["mu"], out=mu, in0=M[b][:, :], scalar1=1.0 / 1024, scalar2=None, op0=ALU.mult)
                DVE("tensor_tensor", ["mu"], ["t2"], out=t2, in0=mu, in1=mu, op=ALU.mult)
                DVE("scalar_tensor_tensor", ["M%d" % (2 + b), "t2"], ["t2"], out=t2, in0=M[2 + b][:, :], scalar=1.0 / 1024, in1=t2, op0=ALU.mult, op1=ALU.subtract)
                ACT("activation", ["t2", "cst"], ["t2"], out=t2, in_=t2, func=AF.Sqrt, bias=cst[:, 1:2], scale=1.0)
                DVE("reciprocal", ["t2"], ["rstd"], out=rstd[:, sl], in_=t2)
                DVE("scalar_tensor_tensor", ["mu", "rstd"], ["nmr"], out=nmr[:, sl], in0=mu, scalar=-1.0, in1=rstd[:, sl], op0=ALU.mult, op1=ALU.mult)
            for cc in range(8):
                def ev_g(ps, bi, gk, cc=cc):
                    sl = slice(512 * bi, 512 * (bi + 1))
                    cv = hT[:, cc, XC + 512 * bi:XC + 512 * (bi + 1)]
                    ACT("activation", [gk], ["gt%d" % bi], out=gt[bi], in_=ps, func=AF.Silu)
                    DVE("tensor_tensor", ["hT%d" % cc, "rstd"], ["tf%d" % bi], out=tf[bi], in0=cv, in1=rstd[:, sl], op=ALU.mult)
                    DVE("tensor_tensor", ["tf%d" % bi, "nmr"], ["tf%d" % bi], out=tf[bi], in0=tf[bi], in1=nmr[:, sl], op=ALU.add)
                    ACT("activation", ["tf%d" % bi, "convp"], ["at%d" % bi], out=at_[bi], in_=tf[bi], func=AF.Silu, scale=convp[:, cc, 32:33], bias=convp[:, cc, 33:34])
                    DVE("tensor_tensor", ["at%d" % bi, "gt%d" % bi], ["ycv%d" % (cc % 2)], out=ycv2[cc % 2][:, sl], in0=at_[bi], in1=gt[bi], op=ALU.mult)
                proj(w_in, OFF_CG + cc * 128, 128, MAINB, ev_g)
                P.dma("sp", mixs[cc * 128:(cc + 1) * 128, :], ycv2[cc % 2], reads=["ycv%d" % (cc % 2)], writes=["mixs"], sem_key="st_ycv%d" % (cc % 2))

        qg = ab(2 * 17120, 2048).rearrange("p (d t) -> p d t", d=2)

        def gla_head_main(h, hm):
            for dc in range(2):
                def ev(ps, bi, gk, dc=dc):
                    copy_op(evac_eng(), qT[:, dc, bi * 512:(bi + 1) * 512], ps, [gk], ["qT"])
                proj(w_in, OFF_Q + h * 256 + dc * 128, 128, MAINB, ev)
            k_proj(h)
            gtasks = []
            gdmas = []
            for ec in range(4):
                def ev(ps, bi, gk, ec=ec):
                    t = ptmp[cnt["ev"] % 2]
                    tk = "ptmp%d" % (cnt["ev"] % 2)
                    cnt["ev"] += 1
                    ACT("activation", [gk], ["junkf"], out=junkf, in_=ps, func=AF.Silu)
                    DVE("tensor_scalar", ["junkf", "convp"], [tk], out=t, in0=junkf, scalar1=convp[:, ec, 34:35], scalar2=None, op0=ALU.mult)
                    tb, tbk = next_tb()
                    for j in range(4):
                        PE("transpose", [tk, "identb"], [tbk], out=tb[:, j, :], in_=t[:, j * 128:(j + 1) * 128], identity=identb[:])
                    DVE("tensor_copy", [tbk], ["ngtok"], out=ngtok[:, bi * 4:(bi + 1) * 4, ec * 128:(ec + 1) * 128], in_=tb[:, 0:4, :])
                d, tk_ = proj_tasks(w_in, OFF_GG + h * 512 + ec * 128, 128, MAINB, ev)
                gdmas.append(d)
                gtasks.extend(tk_)
            gla_prep_all(h, True, v_tasks(h))
            gdmas[0]()
            src_, dst_ = dec, cpA
            srck, dstk = "dec", "cpA"
            for sft in (1, 2, 4, 8):
                DVE("tensor_copy", [srck], [dstk], out=dst_[:, :, 0:sft], in_=src_[:, :, 0:sft])
                DVE("tensor_tensor", [srck], [dstk], out=dst_[:, :, sft:16], in0=src_[:, :, sft:16], in1=src_[:, :, 0:16 - sft], op=ALU.mult)
                if dst_ is cpA:
                    src_, dst_, srck, dstk = cpA, cpB, "cpA", "cpB"
                else:
                    src_, dst_, srck, dstk = cpB, cpA, "cpB", "cpA"
            inc, inck = src_, srck
            DVE("tensor_copy", ["carry"], ["PBf"], out=PBf[:, :, 0], in_=carry[:, 2 * h:2 * h + 2])
            for dc in range(2):
                DVE("tensor_scalar", [inck, "carry"], ["PBf"], out=PBf[:, dc, 1:16], in0=inc[:, dc, 0:15], scalar1=carry[:, 2 * h + dc:2 * h + dc + 1], scalar2=None, op0=ALU.mult)
            DVE("tensor_tensor", [inck, "carry"], ["carry"], out=carry[:, 2 * h:2 * h + 2], in0=inc[:, :, 15], in1=carry[:, 2 * h:2 * h + 2], op=ALU.mult)
            for i in range(8):
                for a in range(2):
                    for dc in range(2):
                        n = 2 * i + a
                        ACT("activation", ["qz", "PBf"], ["qg"], out=qg[:, dc, i * 128 + a * 64:i * 128 + (a + 1) * 64], in_=qz[:, i, a, dc, a * 64:(a + 1) * 64],
                            func=AF.Copy, scale=PBf[:, dc, n:n + 1])
            olo = ab(O_YT, 4096).rearrange("p (i e) -> p i e", i=8)
            gparts = flat_parts(gtasks)

            def gp():
                if gparts:
                    gparts.pop(0)()

            def attn(i):
                tc0 = i * 128
                first = True
                for a in range(2):
                    for dc in range(2):
                        PE("matmul", ["kT", "qz"], ["M0"], M[0][:, 0:128], lhsT=kT[:, dc, tc0:tc0 + 128], rhs=qz[:, i, a, dc, :],
                           start=first, stop=(a == 1 and dc == 1))
                        first = False
                DVE("tensor_tensor", ["M0", "cmask"], ["attnb%d" % (i % 2)], out=attnb[i % 2], in0=M[0][:, 0:128], in1=cmask[:], op=ALU.mult)

            attn(0)
            for i in range(8):
                at, atk = attnb[i % 2], "attnb%d" % (i % 2)
                PE("matmul", [atk, "vtok"], ["M2o"], M[2][:, :], lhsT=at, rhs=vtok[:, i, :], start=True, stop=False)
                for dc in range(2):
                    PE("matmul", ["qz", "stb"], ["M2o"], M[2][:, :], lhsT=qz[:, i, 0, dc, :], rhs=stateb[:, h, dc, :], start=False, stop=False)
                state_update(h, i, 0)
                gp()
                if i + 1 < 8:
                    attn(i + 1)
                gp()
                for dc in range(2):
                    PE("matmul", ["qz", "stb"], ["M2o"], M[2][:, :], lhsT=qz[:, i, 1, dc, :], rhs=stateb[:, h, dc, :], start=False, stop=(dc == 1))
                state_update(h, i, 1)
                copy_op("act", olo[:, i, :], M[2][:, :], ["M2o"], ["olo"])
                gp()
                gp()
            while gparts:
                gp()
            s0 = (hm * 4 + h) * HALF
            P.dma("sp", osc[s0:s0 + HALF, :].rearrange("(i p) e -> p i e", p=128), olo, reads=["olo"], writes=["osc"], sem_key="st_olo")
            P.dma("sp", ngsc[s0:s0 + HALF, :].rearrange("(i p) e -> p i e", p=128), ngtok, reads=["ngtok"], writes=["ngsc"], sem_key="st_ng")
            q0 = (hm * 4 + h) * 256
            P.dma("sp", qgsc[q0:q0 + 256, :].rearrange("(d p) t -> p d t", p=128), qg, reads=["qg"], writes=["qgsc"], sem_key="st_qg")

        def exchange():
            sflat = state32[:].rearrange("p h d e -> p (h d e)")
            P.dma("sp", xsrc[0], sflat[:, 0:2048], reads=["st32"], writes=["xsrc0"], sem_key="st_xs0")
            P.dma("sp", xsrc[1], sflat[:, 2048:4096], reads=["st32"], writes=["xsrc1"], sem_key="st_xs1")
            P.dma("sp", xsrc[2], carry[:], reads=["carry"], writes=["xsrc2"], sem_key="st_xs2")
            for i in range(3):
                P.op("pool", "collective_compute", ["xsrc%d" % i], ["xdst%d" % i], "AllGather", ALU.bypass, replica_groups=[[0, 1, 2, 3], [4, 5, 6, 7]],
                     ins=[xsrc[i]], outs=[xdst[i]], cc=True)

        def start_state_tasks():
            Lh = af(6656, 2048)
            wk = ["w0", "t2w", "w2"]

            def weights():
                for i in range(3):
                    P.dma("sp", pq[:, i, :], xdst[2][i * 128:(i + 1) * 128, 0:8], reads=["xdst2"], writes=["pq%d" % i])
                DVE("tensor_scalar", ["pq1", "selt"], ["t1"], out=wts[:, 3, :], in0=pq[:, 1, :], scalar1=selt[:, 1:2], scalar2=selt[:, 5:6], op0=ALU.mult, op1=ALU.add)
                DVE("tensor_scalar", ["pq2", "selt"], ["t2w"], out=wts[:, 1, :], in0=pq[:, 2, :], scalar1=selt[:, 2:3], scalar2=selt[:, 6:7], op0=ALU.mult, op1=ALU.add)
                DVE("tensor_tensor", ["t1", "t2w"], ["w0"], out=wts[:, 0, :], in0=wts[:, 3, :], in1=wts[:, 1, :], op=ALU.mult)
                DVE("tensor_scalar", ["w0", "selt"], ["w0"], out=wts[:, 0, :], in0=wts[:, 0, :], scalar1=selt[:, 0:1], scalar2=None, op0=ALU.mult)
                DVE("tensor_scalar", ["t2w", "selt"], ["t2w"], out=wts[:, 1, :], in0=wts[:, 1, :], scalar1=selt[:, 1:2], scalar2=None, op0=ALU.mult)
                DVE("memset", [], ["w2"], wts[:, 2, :], 1.0)
                DVE("tensor_scalar", ["w2", "selt"], ["w2"], out=wts[:, 2, :], in0=wts[:, 2, :], scalar1=selt[:, 2:3], scalar2=None, op0=ALU.mult)

            def acc(i, half):
                def task():
                    P.dma("sp", Lh, xdst[half][i * 128:(i + 1) * 128, :], reads=["xdst%d" % half], writes=["Lh"])
                    for q4 in range(4):
                        hd = half * 4 + q4
                        sv = state32[:, hd // 2, hd % 2, :]
                        if i == 0:
                            DVE("tensor_scalar", ["Lh", wk[i]], ["st32"], out=sv, in0=Lh[:, q4 * 512:(q4 + 1) * 512], scalar1=wts[:, i, hd:hd + 1], scalar2=None, op0=ALU.mult)
                        else:
                            DVE("scalar_tensor_tensor", ["Lh", wk[i], "st32"], ["st32"], out=sv, in0=Lh[:, q4 * 512:(q4 + 1) * 512], scalar=wts[:, i, hd:hd + 1], in1=sv,
                                op0=ALU.mult, op1=ALU.add)
                return task

            def casts():
                for h in range(4):
                    for dc in range(2):
                        copy_op("act" if dc else "dve", stateb[:, h, dc, :], state32[:, h, dc, :], ["st32"], ["stb"])
            return [weights] + [acc(i, half) for i in range(3) for half in range(2)] + [casts]

        EB = 10240

        def gla_epilogue_tasks(heads, n_of=1, of_base=None):
            olo = ab(2 * EB, 4096).rearrange("p (i e) -> p i e", i=8)
            ng = ab(2 * EB + 4096, 4096).rearrange("p (i e) -> p i e", i=8)
            qg2 = ab(2 * EB + 8192, 2048).rearrange("p (d t) -> p d t", d=2)
            yT2 = ab(2 * EB + 10240, 4096).rearrange("p (e t) -> p e t", e=4)
            ytk2 = [ab(2 * EB + 14336 + 512 * i, 512) for i in range(2)]
            ofs = [af(EB + 7680, 512)] if n_of == 1 else [af(of_base + 512 * i, 512) for i in range(n_of)]
            jf = af(9728, 512)
            steps = [(h, hm, i) for (h, hm) in heads for i in range(8)]

            def head_loads(k):
                h, hm, i = steps[k]
                s0 = (hm * 4 + h) * HALF
                q0 = (hm * 4 + h) * 256
                P.dma("sp", qg2, qgsc[q0:q0 + 256, :].rearrange("(d p) t -> p d t", p=128), reads=["qgsc"], writes=["qg2"])
                P.dma("sp", olo, osc[s0:s0 + HALF, :].rearrange("(i p) e -> p i e", p=128), reads=["osc"], writes=["olo2"])
                P.dma("sp", ng, ngsc[s0:s0 + HALF, :].rearrange("(i p) e -> p i e", p=128), reads=["ngsc"], writes=["ng2"])

            def front(k):
                h, hm, i = steps[k]
                if i == 0:
                    if k == 0:
                        head_loads(0)
                    DVE("memset", [], ["ss_%d" % c for c in range(8)], ss[:, 0:8], 0.0)
                tc0 = i * 128
                mb, mbk = (M[2], "M2o") if k % 2 == 0 else (M[3], "M3")
                for dc in range(2):
                    PE("matmul", ["qg2", "stb"], [mbk], mb[:, :], lhsT=qg2[:, dc, tc0:tc0 + 128], rhs=stateb[:, h, dc, :], start=(dc == 0), stop=(dc == 1))
                of_, ofk = ofs[k % len(ofs)], "of%d" % (k % len(ofs))
                DVE("tensor_tensor", [mbk, "olo2"], [ofk], out=of_, in0=mb[:, :], in1=olo[:, i, :], op=ALU.add)
                sk, rk = "ss_%d" % i, "rs_%d" % i
                ACT("activation", [ofk, sk], ["junk", sk], out=jf, in_=of_, func=AF.Square, accum_out=ss[:, i:i + 1])
                ACT("activation", [sk, "cst"], [rk], out=ss[:, 8 + i:9 + i], in_=ss[:, i:i + 1], func=AF.Sqrt, bias=cst[:, 1:2], scale=1.0 / 512)
                DVE("reciprocal", [rk], [rk], out=ss[:, 8 + i:9 + i], in_=ss[:, 8 + i:9 + i])
                yt, ytk = ytk2[k % 2], "ytok%d" % (k % 2)
                DVE("scalar_tensor_tensor", [ofk, rk, "ng2"], [ytk], out=yt, in0=of_, scalar=ss[:, 8 + i:9 + i], in1=ng[:, i, :], op0=ALU.mult, op1=ALU.mult)

            def back(k):
                h, hm, i = steps[k]
                tc0 = i * 128
                yt, ytk = ytk2[k % 2], "ytok%d" % (k % 2)
                tb, tbk = next_tb()
                for j in range(4):
                    PE("transpose", [ytk, "identb"], [tbk], out=tb[:, j, :], in_=yt[:, j * 128:(j + 1) * 128], identity=identb[:])
                copy_op("act", yT2[:, :, tc0:tc0 + 128], tb[:, 0:4, :], [tbk], ["yT2"])
                if i == 7:
                    mixs = mixs_all[hm * D:(hm + 1) * D, :]
                    r0 = 1024 + h * 512
                    P.dma("sp", mixs[r0:r0 + 512, :].rearrange("(e p) t -> p e t", p=128), yT2, reads=["yT2"], writes=["mixs%d" % hm], sem_key="st_yT")

            def mk(k):
                def task():
                    if k < len(steps):
                        front(k)
                    if k > 0:
                        back(k - 1)
                    if k + 1 < len(steps) and steps[k + 1][2] == 0:
                        head_loads(k + 1)
                return task
            return [mk(k) for k in range(len(steps) + 1)]

        def mem_bufs(h):
            o = 2048 * 3 * (h % 2)
            mqT = ab(o, 2048).rearrange("p (d t) -> p d t", d=2)
            sgT = ab(o + 2048, 2048).rearrange("p (d t) -> p d t", d=2)
            ymT = ab(o + 4096, 2048).rearrange("p (d t) -> p d t", d=2)
            return mqT, sgT, ymT, "%d" % (h % 2)

        def mem_proj_tasks(h):
            mqT, sgT, ymT, sfx = mem_bufs(h)
            out_ = []
            for dc in range(2):
                def ev(ps, bi, gk, dc=dc):
                    copy_op(evac_eng(), mqT[:, dc, bi * 512:(bi + 1) * 512], ps, [gk], ["mqT" + sfx])
                out_.extend(proj_tasks(w_in, OFF_MQ + h * 256 + dc * 128, 128, MAINB, ev)[1])
            for dc in range(2):
                def ev(ps, bi, gk, dc=dc):
                    ACT("activation", [gk], ["sgT" + sfx], out=sgT[:, dc, bi * 512:(bi + 1) * 512], in_=ps, func=AF.Silu)
                out_.extend(proj_tasks(w_in, OFF_MG + h * 256 + dc * 128, 128, MAINB, ev)[1])
            return out_

        def mem_head(h, hm, tasks=(), extra=(), n_extra=0):
            tasks = list(tasks)
            extra = list(extra)
            mixs = mixs_all[hm * D:(hm + 1) * D, :]
            mqT, sgT, ymT, sfx = mem_bufs(h)
            mqk, sgk, ymk = "mqT" + sfx, "sgT" + sfx, "ymT" + sfx
            Pn = [ab(12288 + 256 * i, 256) for i in range(2)]
            PTs = [ab(12800 + 256 * i, 256).rearrange("p (m t) -> p m t", m=2) for i in range(2)]
            DVE("memset", [], ["sm_%d" % c for c in range(8)], sm[:], 0.0)

            def stage_a(i):
                tc0 = i * 128
                for dc in range(2):
                    PE("matmul", [mqk, "mkT"], ["M0"], M[0][:, 0:256], lhsT=mqT[:, dc, tc0:tc0 + 128], rhs=mkT[:, h * 2 + dc, :], start=(dc == 0), stop=(dc == 1))
                lk = "lst%d" % (i % 2)
                lc = 4 * (i % 2)
                DVE("reduce_max", ["M0"], [lk], out=lst[:, lc:lc + 1], in_=M[0][:, 0:256], axis=AX.X)
                DVE("tensor_scalar", [lk], [lk], out=lst[:, lc + 1:lc + 2], in0=lst[:, lc:lc + 1], scalar1=-0.0625, scalar2=None, op0=ALU.mult)
                pn = Pn[i % 2]
                pk = "Pn%d" % (i % 2)
                ACT("activation", ["M0", lk, "sm_%d" % i], [pk, "sm_%d" % i], out=pn, in_=M[0][:, 0:256], func=AF.Exp, bias=lst[:, lc + 1:lc + 2], scale=0.0625, accum_out=sm[:, i:i + 1])
                DVE("reciprocal", ["sm_%d" % i], [lk], out=lst[:, lc + 2:lc + 3], in_=sm[:, i:i + 1])
                DVE("tensor_scalar", [pk, lk], [pk], out=pn, in0=pn, scalar1=lst[:, lc + 2:lc + 3], scalar2=None, op0=ALU.mult)

            def stage_b(i):
                tc0 = i * 128
                pn = Pn[i % 2]
                pk = "Pn%d" % (i % 2)
                tb, tbk = next_tb()
                for mc in range(2):
                    PE("transpose", [pk, "identb"], [tbk], out=tb[:, mc, :], in_=pn[:, mc * 128:(mc + 1) * 128], identity=identb[:])
                pt = PTs[i % 2]
                ptk = "PTs%d" % (i % 2)
                copy_op("act", pt, tb[:, 0:2, :], [tbk], [ptk])
                for dvc in range(2):
                    for mc in range(2):
                        PE("matmul", ["mv", ptk], ["M1"], M[1][:, dvc * 128:(dvc + 1) * 128], lhsT=mv[:, mc, h * 256 + dvc * 128:h * 256 + (dvc + 1) * 128],
                                                                    rhs=pt[:, mc, :], start=(mc == 0), stop=(mc == 1))
                DVE("tensor_tensor", ["M1", sgk], [ymk], out=ymT[:, :, tc0:tc0 + 128], in0=M[1][:, 0:256].rearrange("p (d t) -> p d t", d=2), in1=sgT[:, :, tc0:tc0 + 128], op=ALU.mult)

            mparts = flat_parts(tasks)

            def mp():
                if mparts:
                    mparts.pop(0)()
            stage_a(0)
            for i in range(8):
                mp()
                if i + 1 < 8:
                    stage_a(i + 1)
                mp()
                mp()
                stage_b(i)
                mp()
                for _ in range(n_extra):
                    if extra:
                        extra.pop(0)()
            while mparts:
                mp()
            for t in extra:
                t()
            r0 = 3072 + h * 256
            P.dma("sp", mixs[r0:r0 + 256, :].rearrange("(e p) t -> p e t", p=128), ymT, reads=[ymk], writes=["mixs"], sem_key="st_ymT" + sfx)

        gbc = state32[:].rearrange("p h d e -> p (h d e)")
        bbc2 = [wbuf[i][:].rearrange("p a b -> p (a b)").bitcast(F32) for i in range(2)]

        def load_ln_params():
            P.dma("sp", gbc, lngb[0:1, :].partition_broadcast(128), reads=["stb"], writes=["st32"], sem_key="ld_gbc")
            for i in range(2):
                P.dma("sp", bbc2[i], lngb[1:2, i * 2048:(i + 1) * 2048].partition_broadcast(128), writes=["wbuf%d_%d" % (i, g) for g in range(4)], sem_key="ld_bbc%d" % i)

        def load_mixT(hm, groups):
            mixT = xT[:, :, XC:XW]
            mixs = mixs_all[hm * D:(hm + 1) * D, :]
            for g in groups:
                P.dma("sp", mixT[:, g * 8:(g + 1) * 8, :], mixs[g * 1024:(g + 1) * 1024, :].rearrange("(kc p) t -> p kc t", p=128),
                      reads=["mixs%d" % hm, "mixs"], writes=["xT", "xT_%d" % g], sem_key="ld_mix%d" % g)

        def out1(hm, tasks=(), per_nb=0, groups=(0, 1, 2, 3)):
            tasks = list(tasks)
            row0 = XC + hm * HALF
            orow0 = hm * HALF
            s1, s2 = s1h[hm], s2h[hm]
            s1k, s2k = "s1_%d" % hm, "s2_%d" % hm
            mixT = xT[:, :, XC:XW]
            mixs = mixs_all[hm * D:(hm + 1) * D, :]
            load_mixT(hm, groups)
            wo = [ab(8192 * i, 8192).rearrange("p (k n) -> p k n", k=32) for i in range(2)]
            xr = [af(8192 + 256 * i, 256) for i in range(3)]
            yb = [af(8960 + 256 * i, 256) for i in range(3)]
            jk = af(9728, 256)
            DVE("memset", [], [s1k], s1[:], 0.0)
            DVE("memset", [], [s2k], s2[:], 0.0)
            q = 0
            for nb in range(16):
                w = wo[nb % 2]
                wk = "wo%d" % (nb % 2)
                wv = w_out[:, nb * 256:(nb + 1) * 256].rearrange("(kc p) n -> p kc n", p=128)
                for g4 in range(4):
                    P.dma("pool", w[:, g4 * 8:(g4 + 1) * 8, :], wv[:, g4 * 8:(g4 + 1) * 8, :], writes=["%s_%d" % (wk, g4)])
                for i in range(8):
                    gi = cnt["g"] % 2
                    cnt["g"] += 1
                    gk = "G%d" % gi
                    for kc in range(32):
                        PE("matmul", ["%s_%d" % (wk, kc // 8), "xT_%d" % (kc // 8)], [gk], G[gi][:, 0:256], lhsT=mixT[:, kc, i * 128:(i + 1) * 128], rhs=w[:, kc, :],
                           start=(kc == 0), stop=(kc == 31))
                    x_, xk = xr[q % 3], "xr%d" % (q % 3)
                    y_, yk = yb[q % 3], "yb%d" % (q % 3)
                    for q2 in ([0, 1, 2] if q == 0 else [q + 2]):
                        if q2 < 128:
                            nb2, i2 = q2 // 8, q2 % 8
                            P.dma("act", xr[q2 % 3], xall[row0 + i2 * 128:row0 + (i2 + 1) * 128, nb2 * 256:(nb2 + 1) * 256], writes=["xr%d" % (q2 % 3)])
                    q += 1
                    col = i * 16 + nb
                    DVE("scalar_tensor_tensor", [gk, xk, s1k], [yk, s1k], out=y_, in0=x_, scalar=ALPHA, in1=G[gi][:, 0:256], op0=ALU.mult, op1=ALU.add,
                        accum_out=s1[:, col:col + 1])
                    ACT("activation", [yk, s2k], ["junk", s2k], out=jk, in_=y_, func=AF.Square, accum_out=s2[:, col:col + 1])
                    P.dma("act", out[orow0 + i * 128:orow0 + (i + 1) * 128, nb * 256:(nb + 1) * 256], y_, reads=[yk], writes=["out%d_%d" % (hm, i)],
                          sem_key="st_yb%d" % (q % 3))
                    if tasks and per_nb and (i + 1) % (8 // per_nb) == 0:
                        tasks.pop(0)()
                if tasks and not per_nb and nb % 2 == 1:
                    tasks.pop(0)()
            for t in tasks:
                t()

        def out2_tasks(hm, yf, defer):
            orow0 = hm * HALF
            s1, s2 = s1h[hm], s2h[hm]
            s1k, s2k = "s1_%d" % hm, "s2_%d" % hm
            nbuf = len(yf)
            pend = []

            def store(i):
                y_, yk = yf[i % nbuf], "yf%d" % (i % nbuf)
                P.dma("sp", out[orow0 + i * 128:orow0 + (i + 1) * 128, :], y_, reads=[yk], writes=["fin%d_%d" % (hm, i)], sem_key="st_yf%d" % (i % nbuf))

            def load(i):
                y_, yk = yf[i % nbuf], "yf%d" % (i % nbuf)
                P.dma("sp", y_, out[orow0 + i * 128:orow0 + (i + 1) * 128, :], reads=["out%d_%d" % (hm, i)], writes=[yk])

            def part(i, c):
                def task():
                    y_, yk = yf[i % nbuf], "yf%d" % (i % nbuf)
                    lo = 6 * (i % 2)
                    lk = "lsu%d" % (i % 2)
                    if c == 0:
                        while pend:
                            store(pend.pop(0))
                        if defer:
                            if i == 0:
                                load(0)
                        else:
                            for t_ in ([0, 1, 2] if i == 0 else [i + 2]):
                                if t_ < 8:
                                    load(t_)
                        DVE("reduce_sum", [s1k], [lk], out=lsu[:, lo:lo + 1], in_=s1[:, i * 16:(i + 1) * 16], axis=AX.X)
                        DVE("reduce_sum", [s2k], [lk], out=lsu[:, lo + 1:lo + 2], in_=s2[:, i * 16:(i + 1) * 16], axis=AX.X)
                        DVE("tensor_scalar", [lk], [lk], out=lsu[:, lo + 2:lo + 3], in0=lsu[:, lo:lo + 1], scalar1=1.0 / D, scalar2=None, op0=ALU.mult)
                        DVE("tensor_tensor", [lk], [lk], out=lsu[:, lo + 3:lo + 4], in0=lsu[:, lo + 2:lo + 3], in1=lsu[:, lo + 2:lo + 3], op=ALU.mult)
                        DVE("scalar_tensor_tensor", [lk], [lk], out=lsu[:, lo + 3:lo + 4], in0=lsu[:, lo + 1:lo + 2], scalar=1.0 / D, in1=lsu[:, lo + 3:lo + 4],
                            op0=ALU.mult, op1=ALU.subtract)
                        ACT("activation", [lk, "cst"], [lk], out=lsu[:, lo + 4:lo + 5], in_=lsu[:, lo + 3:lo + 4], func=AF.Sqrt, bias=cst[:, 1:2], scale=1.0)
                        DVE("reciprocal", [lk], [lk], out=lsu[:, lo + 4:lo + 5], in_=lsu[:, lo + 4:lo + 5])
                        DVE("scalar_tensor_tensor", [lk], [lk], out=lsu[:, lo + 5:lo + 6], in0=lsu[:, lo + 2:lo + 3], scalar=-1.0, in1=lsu[:, lo + 4:lo + 5],
                            op0=ALU.mult, op1=ALU.mult)
                    cs = slice(c * 1024, (c + 1) * 1024)
                    ykc = "%s_c%d" % (yk, c)
                    ACT("activation", [yk, lk], [ykc], out=y_[:, cs], in_=y_[:, cs], func=AF.Identity, scale=lsu[:, lo + 4:lo + 5], bias=lsu[:, lo + 5:lo + 6])
                    DVE("tensor_tensor", [ykc, "st32"], [ykc], out=y_[:, cs], in0=y_[:, cs], in1=gbc[:, cs], op=ALU.mult)
                    DVE("tensor_tensor", [ykc, "wbuf%d_0" % (c // 2)], [ykc, yk] if c == 3 else [ykc], out=y_[:, cs], in0=y_[:, cs],
                        in1=bbc2[c // 2][:, (c % 2) * 1024:(c % 2 + 1) * 1024], op=ALU.add)
                    if c == 3:
                        if defer:
                            pend.append(i)
                        else:
                            store(i)
                    if defer and c == 2 and i + 1 < 8:
                        load(i + 1)
                return task

            def flush():
                while pend:
                    store(pend.pop(0))
            return [part(i, c) for i in range(8) for c in range(4)] + [flush]

        for hm in range(n_main):
            P.barrier()
            build_xT(XC + hm * HALF, 8, XC)
            if hm == 0:
                build_xT(0, 1, 0, rows_per=XC)
            P.barrier()
            conv_phase(hm)
            P.barrier()
            lr_proj()
            gla_zero_pads()
            for h in range(4):
                gla_head_main(h, hm)
            P.barrier()
            if hm == n_main - 1:
                exchange()
            for t in mem_proj_tasks(0):
                t()
            if hm < n_main - 1:
                for h in range(4):
                    mem_head(h, hm, mem_proj_tasks(h + 1) if h < 3 else ())
            else:
                ep0 = gla_epilogue_tasks([(h, 0) for h in range(4)], n_of=2, of_base=8704)
                mem_head(0, hm, mem_proj_tasks(1))
                mem_head(1, hm, mem_proj_tasks(2), start_state_tasks(), 1)
                mem_head(2, hm, mem_proj_tasks(3), ep0[:17], 3)
                load_mixT(0, [0])
                mem_head(3, hm, (), ep0[17:], 2)
                load_mixT(0, [1, 2, 3])
        P.barrier()
        load_ln_params()
        out1(0, gla_epilogue_tasks([(h, 1) for h in range(4)]), per_nb=2, groups=())
        P.barrier()
        out1(1, out2_tasks(0, [af(EB, 4096), af(EB + 4096, 4096)], True), per_nb=2)
        P.barrier()
        for t in out2_tasks(1, [af(0, 4096), af(4096, 4096), af(8192, 4096), af(12288, 4096)], False):
            t()
        if dbg:
            pass
        P.barrier()
        P.op("sp", None)
        P.emit(block, st)
        build.n_ops = len(P.ops)
        build.n_sems = P.n_sems
    return nc


def _consts():
    ident = np.eye(128, dtype=np.float32)
    s = np.arange(128)[:, None]
    t = np.arange(128)[None, :]
    same = (s // 64) == (t // 64)
    tri = (same & (s <= t)).astype(np.float32)
    aft = (same & (s > t)).astype(np.float32)
    m2 = np.concatenate([tri, aft], axis=1) * np.float32(-1.0 / 16.0)
    return ident, np.ascontiguousarray(m2.astype(np.float32)), tri


def make_in_maps(x, mem, w_in, conv_dw, conv_dw_b, conv_ln_g, conv_ln_b, gla_w_gate, gla_gate_b, gla_norm_g,
                 w_mem_kv, w_out, ln_g, ln_b):
    f = lambda a: np.ascontiguousarray(np.asarray(a, dtype=np.float32))
    x = f(x); mem = f(mem)
    ident, m2, cm = _consts()
    prm = np.zeros((35, 1024), np.float32)
    prm[0:31] = f(conv_dw)[0]
    prm[31] = f(conv_dw_b)[0]
    prm[32] = f(conv_ln_g)[0]
    prm[33] = f(conv_ln_b)[0]
    prm[34, 0:512] = f(gla_norm_g)[0]
    wga = np.concatenate([f(gla_w_gate)[0], f(gla_gate_b)[0][None, :]], axis=0)
    lngb = np.stack([f(ln_g)[0], f(ln_b)[0]], axis=0)
    w_in0, w_mkv0, w_out0 = f(w_in)[0], f(w_mem_kv)[0], f(w_out)[0]
    maps = []
    for j in range(8):
        b, r = j // 4, j % 4
        xa = np.zeros((XC + NQ, D), np.float32)
        xa[XC:] = x[b, r * NQ:(r + 1) * NQ]
        if r > 0:
            xa[0:XC] = x[b, r * NQ - XC:r * NQ]
        sel = np.zeros((128, 8), np.float32)
        for i in range(4):
            sel[:, i] = 1.0 if i < r else 0.0
            sel[:, 4 + i] = 0.0 if i < r else 1.0
        maps.append({"sel": sel, "xall": xa, "mem": mem[b], "w_in": w_in0, "w_mkv": w_mkv0, "w_out": w_out0, "prm": prm, "wga": f(wga),
                     "lngb": f(lngb), "identd": ident, "m2": m2, "cmaskd": cm})
    return maps


def kernel(**inputs):
    nc = build()
    maps = make_in_maps(**inputs)
    res = run_bass_kernel_spmd(nc, maps, core_ids=list(range(8)))
    outs = [np.asarray(r["out"], dtype=np.float32) for r in res.results]
    y = np.stack([np.concatenate(outs[0:4], axis=0), np.concatenate(outs[4:8], axis=0)], axis=0)
    return y
```
